# Optimizing a Trainium2 kernel written in Bass

```python
import math
import jax, jax.numpy as jnp
from jax import lax
import numpy as np

D_MODEL = 1024
BATCH = 2
SEQ = 8192
DEPTH = 4
DEC_BATCH = 32
DEC_SEQ = 8
PAST_LEN = 8192
PAGE_SIZE = 128

N_MIXERS = 2
N_ATTN_LAYERS = (DEPTH + 1) // 2
N_RNN_LAYERS = DEPTH // 2
WINDOWS = (128, 512, 2048)
DILATIONS = (1, 4, 16)
N_GROUPS = 3
HEADS_PER_GROUP = 8
HEAD_DIM = 128
N_HEADS = N_GROUPS * HEADS_PER_GROUP
QKV_WIDTH = N_HEADS * HEAD_DIM
ATTN_WIDTH = HEADS_PER_GROUP * HEAD_DIM
Q_BLK = 128
N_BUCKETS = 32
MAX_DISTANCE = 2048
D_RNN = 1280
RNN_BLOCKS = 10
RNN_BLOCK_W = D_RNN // RNN_BLOCKS
CONV_W = 4
LRU_C = 8.0
EPS = 1e-6
NEG = -1e30

kernel_name = 'hybrid_dilated_attn_rglru_step'


def rms_norm(x, g):
    xf = x.astype(jnp.float32)
    y = xf * lax.rsqrt(jnp.mean(xf * xf, axis=-1, keepdims=True) + EPS)
    return (y * g.astype(jnp.float32)).astype(x.dtype)


def t5_bucket(dist):
    n = jnp.maximum(dist, 0)
    max_exact = N_BUCKETS // 2
    nf = jnp.maximum(n, 1).astype(jnp.float32)
    large = max_exact + (jnp.log(nf / max_exact) / math.log(MAX_DISTANCE / max_exact)
                         * (N_BUCKETS - max_exact)).astype(jnp.int32)
    large = jnp.minimum(large, N_BUCKETS - 1)
    return jnp.where(n < max_exact, n, large)


def attn_project(x, norm_g, w_in, q_g, k_g):
    h = rms_norm(x, norm_g)
    proj = h @ w_in
    q, k, v, gate = jnp.split(proj, [QKV_WIDTH, 2 * QKV_WIDTH, 3 * QKV_WIDTH], axis=-1)
    shp = x.shape[:2] + (N_HEADS, HEAD_DIM)
    q = rms_norm(q.reshape(shp), q_g)
    k = rms_norm(k.reshape(shp), k_g)
    return q, k, v.reshape(shp), gate


def dilated_band_prompt(q, k, v, bias_tab, dil, n_keys):
    B, S, H, Dh = q.shape
    span = dil * Q_BLK
    Sp = -(-S // span) * span
    L = Sp // dil
    nb = L // Q_BLK

    def to_blocks(t):
        t = jnp.pad(t, ((0, 0), (0, Sp - S), (0, 0), (0, 0)))
        t = t.reshape(B, L, dil, H, Dh).transpose(0, 2, 1, 3, 4)
        return t.reshape(B, dil, nb, Q_BLK, H, Dh)

    def band(t):
        prev = jnp.pad(t, ((0, 0), (0, 0), (1, 0), (0, 0), (0, 0), (0, 0)))[:, :, :-1]
        return jnp.concatenate([prev, t], axis=3)

    qb = to_blocks(q)
    kk = band(to_blocks(k))
    vv = band(to_blocks(v))
    a = jnp.arange(Q_BLK)[:, None]
    bcol = jnp.arange(2 * Q_BLK)[None, :]
    step = a + Q_BLK - bcol
    in_band = (step >= 0) & (step <= n_keys)
    bias = bias_tab[t5_bucket(step * dil)]
    bias = jnp.where(in_band[..., None], bias, NEG).transpose(2, 0, 1)
    exists = (jnp.arange(nb)[:, None] > 0) | (bcol >= Q_BLK)
    s = jnp.einsum('brnqhd,brnkhd->brnhqk', qb, kk) * (HEAD_DIM ** -0.5) + bias
    s = jnp.where(exists[:, None, None, :], s, NEG)
    lse = jax.nn.logsumexp(s, axis=-1)
    p = jnp.exp(s - lse[..., None])
    o = jnp.einsum('brnhqk,brnkhd->brnqhd', p, vv)
    o = o.reshape(B, dil, L, H, Dh).transpose(0, 2, 1, 3, 4).reshape(B, Sp, H, Dh)[:, :S]
    lse = lse.transpose(0, 1, 2, 4, 3).reshape(B, dil, L, H).transpose(0, 2, 1, 3).reshape(B, Sp, H)[:, :S]
    return o, lse


def dilated_window_decode(q, k_ext, v_ext, bias_tab, dil, n_keys):
    T = q.shape[1]
    Wb = k_ext.shape[1] - T
    j = jnp.arange(n_keys + 1)
    idx = Wb + jnp.arange(T)[:, None] - j[None, :] * dil
    valid = idx >= 0
    idx = jnp.maximum(idx, 0)
    kg = k_ext[:, idx]
    vg = v_ext[:, idx]
    bias = bias_tab[t5_bucket(j * dil)].T
    s = jnp.einsum('bthd,btjhd->bhtj', q, kg) * (HEAD_DIM ** -0.5) + bias[None, :, None, :]
    s = jnp.where(valid[None, None], s, NEG)
    lse = jax.nn.logsumexp(s, axis=-1)
    p = jnp.exp(s - lse[..., None])
    o = jnp.einsum('bhtj,btjhd->bthd', p, vg)
    return o, lse.transpose(0, 2, 1)


def merge_groups(outs, lses):
    o = jnp.stack(outs)
    w = jax.nn.softmax(jnp.stack(lses), axis=0)
    return jnp.einsum('gbsh,gbshd->bshd', w, o)


def attn_layer_prompt(x, norm_g, w_in, q_g, k_g, w_out, rel_bias):
    B, S, _ = x.shape
    q, k, v, gate = attn_project(x, norm_g, w_in, q_g, k_g)
    qf, kf, vf = q.astype(jnp.float32), k.astype(jnp.float32), v.astype(jnp.float32)
    outs, lses, bufs = [], [], []
    for g in range(N_GROUPS):
        hs = slice(g * HEADS_PER_GROUP, (g + 1) * HEADS_PER_GROUP)
        o, l = dilated_band_prompt(qf[:, :, hs], kf[:, :, hs], vf[:, :, hs], rel_bias[:, hs],
                                   DILATIONS[g], WINDOWS[g] // DILATIONS[g])
        outs.append(o)
        lses.append(l)
        keep = min(WINDOWS[g], S)
        bufs.append(jnp.stack([k[:, S - keep:, hs], v[:, S - keep:, hs]], axis=2))
    merged = merge_groups(outs, lses).astype(x.dtype).reshape(B, S, ATTN_WIDTH)
    y = x + (jax.nn.silu(gate) * merged) @ w_out
    return y, bufs


def attn_layer_sample(x, caches, norm_g, w_in, q_g, k_g, w_out, rel_bias):
    B, T, _ = x.shape
    q, k, v, gate = attn_project(x, norm_g, w_in, q_g, k_g)
    qf = q.astype(jnp.float32)
    outs, lses, bufs = [], [], []
    for g in range(N_GROUPS):
        hs = slice(g * HEADS_PER_GROUP, (g + 1) * HEADS_PER_GROUP)
        cache = caches[g]
        Wb = cache.shape[1]
        kv_new = jnp.stack([k[:, :, hs], v[:, :, hs]], axis=2).astype(cache.dtype)
        ext = jnp.concatenate([cache, kv_new], axis=1)
        extf = ext.astype(jnp.float32)
        o, l = dilated_window_decode(qf[:, :, hs], extf[:, :, 0], extf[:, :, 1], rel_bias[:, hs],
                                     DILATIONS[g], WINDOWS[g] // DILATIONS[g])
        outs.append(o)
        lses.append(l)
        bufs.append(ext[:, T:T + Wb])
    merged = merge_groups(outs, lses).astype(x.dtype).reshape(B, T, ATTN_WIDTH)
    y = x + (jax.nn.silu(gate) * merged) @ w_out
    return y, bufs


def causal_dwconv(xpad, w, b):
    out = lax.conv_general_dilated(xpad, w[:, None, :], window_strides=(1,), padding='VALID',
                                   dimension_numbers=('NWC', 'WIO', 'NWC'),
                                   feature_group_count=D_RNN)
    return out + b


def rglru(xc, h0, ga_w, ga_b, gx_w, gx_b, lam):
    B, L, _ = xc.shape
    xf = xc.astype(jnp.float32)
    xb = xf.reshape(B, L, RNN_BLOCKS, RNN_BLOCK_W)
    r = jax.nn.sigmoid(jnp.einsum('blhi,hij->blhj', xb, ga_w.astype(jnp.float32))
                       + ga_b.astype(jnp.float32)).reshape(B, L, D_RNN)
    i = jax.nn.sigmoid(jnp.einsum('blhi,hij->blhj', xb, gx_w.astype(jnp.float32))
                       + gx_b.astype(jnp.float32)).reshape(B, L, D_RNN)
    log_a = -LRU_C * r * jax.nn.softplus(-lam.astype(jnp.float32))
    a = jnp.exp(log_a)
    bterm = jnp.sqrt(-jnp.expm1(2.0 * log_a)) * (i * xf)
    bterm = bterm.at[:, 0].add(a[:, 0] * h0.astype(jnp.float32))

    def comb(lhs, rhs):
        a1, b1 = lhs
        a2, b2 = rhs
        return a1 * a2, a2 * b1 + b2

    _, h = lax.associative_scan(comb, (a, bterm), axis=1)
    return h


def rnn_layer(x, conv_state, h0, norm_g, w_in, conv_w, conv_b, ga_w, ga_b, gx_w, gx_b, lam, w_out):
    hn = rms_norm(x, norm_g)
    xb, gate = jnp.split(hn @ w_in, [D_RNN], axis=-1)
    xcat = jnp.concatenate([conv_state.astype(xb.dtype), xb], axis=1)
    xc = causal_dwconv(xcat, conv_w, conv_b)
    h = rglru(xc, h0, ga_w, ga_b, gx_w, gx_b, lam)
    y = x + (jax.nn.silu(gate) * h.astype(x.dtype)) @ w_out
    return y, h[:, -1].astype(x.dtype), xcat[:, -(CONV_W - 1):]


def setup_inputs(seed: int = 0) -> dict:
    key = jax.random.key(seed)
    ks = iter(jax.random.split(key, 40))

    def nrm(shape, scale):
        return jax.random.normal(next(ks), shape, jnp.float32) * scale

    NA, NR = N_ATTN_LAYERS, N_RNN_LAYERS
    in_w = 3 * QKV_WIDTH + ATTN_WIDTH
    u = jax.random.uniform(next(ks), (NR, D_RNN), jnp.float32, 0.9, 0.999)
    a_base = u ** (1.0 / LRU_C)
    lam = jnp.log(a_base) - jnp.log1p(-a_base)
    kv_shape = lambda w: (NA, DEC_BATCH, min(w, PAST_LEN), 2, HEADS_PER_GROUP, HEAD_DIM)
    return {
        'x_prompt': nrm((BATCH, SEQ, D_MODEL), 1.0),
        'x_sample': nrm((DEC_BATCH, DEC_SEQ, D_MODEL), 1.0),
        'cache_kv_w128': nrm(kv_shape(WINDOWS[0]), 1.0),
        'cache_kv_w512': nrm(kv_shape(WINDOWS[1]), 1.0),
        'cache_kv_w2048': nrm(kv_shape(WINDOWS[2]), 1.0),
        'state_rglru_h': nrm((NR, DEC_BATCH, D_RNN), 0.5),
        'state_rglru_conv': nrm((NR, DEC_BATCH, CONV_W - 1, D_RNN), 1.0),
        'attn_norm': 1.0 + nrm((NA, D_MODEL), 0.05),
        'attn_w_in': nrm((NA, D_MODEL, in_w), D_MODEL ** -0.5),
        'attn_q_norm': 1.0 + nrm((NA, HEAD_DIM), 0.05),
        'attn_k_norm': 1.0 + nrm((NA, HEAD_DIM), 0.05),
        'attn_w_out': nrm((NA, ATTN_WIDTH, D_MODEL), ATTN_WIDTH ** -0.5),
        'rel_bias': nrm((N_BUCKETS, N_HEADS), 0.5),
        'rnn_norm': 1.0 + nrm((NR, D_MODEL), 0.05),
        'rnn_w_in': nrm((NR, D_MODEL, 2 * D_RNN), D_MODEL ** -0.5),
        'rnn_conv_w': nrm((NR, CONV_W, D_RNN), CONV_W ** -0.5),
        'rnn_conv_b': nrm((NR, D_RNN), 0.02),
        'rnn_gate_a_w': nrm((NR, RNN_BLOCKS, RNN_BLOCK_W, RNN_BLOCK_W), RNN_BLOCK_W ** -0.5),
        'rnn_gate_a_b': nrm((NR, RNN_BLOCKS, RNN_BLOCK_W), 0.1),
        'rnn_gate_x_w': nrm((NR, RNN_BLOCKS, RNN_BLOCK_W, RNN_BLOCK_W), RNN_BLOCK_W ** -0.5),
        'rnn_gate_x_b': nrm((NR, RNN_BLOCKS, RNN_BLOCK_W), 0.1),
        'rnn_lambda': lam,
        'rnn_w_out': nrm((NR, D_RNN, D_MODEL), D_RNN ** -0.5),
    }


def reference(x_prompt, x_sample, cache_kv_w128, cache_kv_w512, cache_kv_w2048, state_rglru_h,
              state_rglru_conv, attn_norm, attn_w_in, attn_q_norm, attn_k_norm, attn_w_out, rel_bias,
              rnn_norm, rnn_w_in, rnn_conv_w, rnn_conv_b, rnn_gate_a_w, rnn_gate_a_b, rnn_gate_x_w,
              rnn_gate_x_b, rnn_lambda, rnn_w_out):
    cache_groups = (cache_kv_w128, cache_kv_w512, cache_kv_w2048)
    yp, ys = x_prompt, x_sample
    kv_p = [[] for _ in range(N_GROUPS)]
    kv_s = [[] for _ in range(N_GROUPS)]
    h_p, h_s, c_p, c_s = [], [], [], []
    for i in range(DEPTH):
        li = i // N_MIXERS
        if i % N_MIXERS == 0:
            aw = (attn_norm[li], attn_w_in[li], attn_q_norm[li], attn_k_norm[li], attn_w_out[li], rel_bias)
            yp, bp = attn_layer_prompt(yp, *aw)
            ys, bs = attn_layer_sample(ys, [c[li] for c in cache_groups], *aw)
            for g in range(N_GROUPS):
                kv_p[g].append(bp[g])
                kv_s[g].append(bs[g])
        else:
            rw = (rnn_norm[li], rnn_w_in[li], rnn_conv_w[li], rnn_conv_b[li], rnn_gate_a_w[li],
                  rnn_gate_a_b[li], rnn_gate_x_w[li], rnn_gate_x_b[li], rnn_lambda[li], rnn_w_out[li])
            zc = jnp.zeros((yp.shape[0], CONV_W - 1, D_RNN), yp.dtype)
            zh = jnp.zeros((yp.shape[0], D_RNN), jnp.float32)
            yp, hp, cp = rnn_layer(yp, zc, zh, *rw)
            ys, hs_, cs = rnn_layer(ys, state_rglru_conv[li], state_rglru_h[li], *rw)
            h_p.append(hp)
            h_s.append(hs_)
            c_p.append(cp)
            c_s.append(cs)
    return (yp, ys,
            jnp.stack(kv_p[0]), jnp.stack(kv_s[0]),
            jnp.stack(kv_p[1]), jnp.stack(kv_s[1]),
            jnp.stack(kv_p[2]), jnp.stack(kv_s[2]),
            jnp.stack(h_p), jnp.stack(h_s),
            jnp.stack(c_p), jnp.stack(c_s))
```

```python
import contextlib
import math
import numpy as np
import concourse.bass as bass
import concourse.mybir as mybir
from concourse.bass_utils import run_bass_kernel_spmd

F32, BF16, I32 = mybir.dt.float32, mybir.dt.bfloat16, mybir.dt.int32
AF = mybir.ActivationFunctionType
ALU = mybir.AluOpType
AX = mybir.AxisListType

NCORE = 8
NP, NS, NSQ = 2048, 32, 4
NT = NP + NS
NX = NT + 3
DIL = (1, 4, 16)
WB = (128, 512, 2048)
NRES = (1, 4, 8)
NQ = (8, 2, 1)
HPB = (1, 4, 16)
HB0 = (0, 1, 5)
NHB = 21
LX = 3 + NP + NSQ * 11
EPS = 1e-6
NEG = -1e30
SCALE = 128 ** -0.5
NDS = 40
TILES = [(0, 512), (512, 512), (1024, 512), (1536, 512), (2048, NX - 2048)]
XTILES = [(0, 512), (512, 512), (1024, 512), (1536, 512), (2048, LX - 2048)]


class Sched:
    def __init__(self, nc, es):
        self.nc = nc
        self.eng = {'pe': nc.tensor, 'act': nc.scalar, 'dve': nc.vector, 'pool': nc.gpsimd, 'sp': nc.sync}
        self.sem = {k: es.enter_context(nc.semaphore("s_" + k)) for k in self.eng}
        self.cnt = {k: 0 for k in self.eng}
        self.dsem = [es.enter_context(nc.semaphore("d%d" % i)) for i in range(NDS)]
        self.dcnt = [0] * NDS
        self.dnext = 0
        self.waited = {k: {} for k in self.eng}
        self.state = {}

    def _wait(self, e, tok):
        sem, val, sid = tok
        if self.waited[e].get(sid, 0) >= val:
            return
        self.eng[e].wait_ge(sem, val)
        self.waited[e][sid] = val

    def _deps(self, e, reads, writes):
        for k in reads:
            st = self.state.get(k)
            if st and st['w']:
                self._wait(e, st['w'])
        for k in writes:
            st = self.state.get(k)
            if st:
                if st['w']:
                    self._wait(e, st['w'])
                for t in st['r'].values():
                    self._wait(e, t)

    def _commit(self, tok, reads, writes):
        for k in reads:
            st = self.state.setdefault(k, {'w': None, 'r': {}})
            st['r'][tok[2]] = tok
        for k in writes:
            self.state[k] = {'w': tok, 'r': {}}

    def op(self, e, fn, reads=(), writes=()):
        self._deps(e, reads, writes)
        ins = fn(self.eng[e])
        self.cnt[e] += 1
        ins.then_inc(self.sem[e], 1)
        tok = (self.sem[e], self.cnt[e], e)
        self._commit(tok, reads, writes)

    def dma(self, e, fn, reads=(), writes=(), inc=16):
        self._deps(e, reads, writes)
        i = self.dnext
        self.dnext = (i + 1) % NDS
        if self.dcnt[i] > 0:
            self._wait(e, (self.dsem[i], self.dcnt[i], 'd%d' % i))
        self.dcnt[i] += inc
        ins = fn(self.eng[e])
        ins.then_inc(self.dsem[i], inc)
        tok = (self.dsem[i], self.dcnt[i], 'd%d' % i)
        self._commit(tok, reads, writes)

    def barrier(self):
        toks = [(self.sem[k], self.cnt[k], k) for k in self.eng if self.cnt[k] > 0]
        toks += [(self.dsem[i], self.dcnt[i], 'd%d' % i) for i in range(NDS) if self.dcnt[i] > 0]
        for e in self.eng:
            for t in toks:
                self._wait(e, t)
        self.state = {}


class Arena:
    def __init__(self, ap, nwords):
        self.ap, self.n, self.top = ap, nwords, 0

    def _alloc(self, nbytes):
        w = (nbytes + 31) // 32 * 8
        off = self.top
        self.top += w
        assert self.top <= self.n, ("arena overflow", self.top, self.n)
        return off, w

    def f32(self, n):
        off, w = self._alloc(4 * n)
        return self.ap[:, off:off + n]

    def i32(self, n):
        off, w = self._alloc(4 * n)
        return self.ap[:, off:off + n].bitcast(I32)

    def bf16(self, n):
        off, w = self._alloc(2 * n)
        return self.ap[:, off:off + w].bitcast(BF16)[:, 0:n]


def build_program():
    nc = bass.Bass("TRN2", target_bir_lowering=False)

    def din(name, shape, dt=F32):
        return nc.dram_tensor(name, list(shape), dt, kind="ExternalInput").ap()

    def dout(name, shape, dt=F32):
        return nc.dram_tensor(name, list(shape), dt, kind="ExternalOutput").ap()

    def dscr(name, shape, dt=BF16):
        return nc.dram_tensor(name, list(shape), dt, kind="Internal").ap()

    xT_in = din("xT_in", [128, 8, NT])
    cache = [din("cache%d" % g, [2, NSQ, WB[g], 2, 8, 128]) for g in range(3)]
    h0s_in = din("h0s", [2, 128, 10, 4])
    cst_in = din("cst", [2, 128, 10, 4, 3])
    a_norm = din("a_norm", [2, 128, 8])
    a_win = din("a_win", [2, 1024, 10240])
    a_qn = din("a_qn", [2, 128, 128])
    a_kn = din("a_kn", [2, 128, 128])
    a_wout = din("a_wout", [2, 1024, 1024])
    relb = din("relb", [32, 24])
    r_norm = din("r_norm", [2, 128, 8])
    r_win = din("r_win", [2, 1024, 2560])
    r_cw = din("r_cw", [2, 128, 10, 4])
    r_cb = din("r_cb", [2, 128, 10])
    r_gaw = din("r_gaw", [2, 10, 128, 128])
    r_gab = din("r_gab", [2, 128, 10])
    r_gxw = din("r_gxw", [2, 10, 128, 128])
    r_gxb = din("r_gxb", [2, 128, 10])
    r_lam = din("r_lam", [2, 128, 10])
    r_wout = din("r_wout", [2, 1280, 1024])
    ohr_in = din("ohr", [33, 3, 384])
    negrow = din("negrow", [1, 24])
    meta_f = din("meta_f", [128, 12])
    meta_i = din("meta_i", [128, 2], I32)
    yT_out = dout("yT_out", [128, 8, NT])
    kvp = [dout("kvp%d" % g, [2, WB[g], 2, 8, 128]) for g in range(3)]
    kvs = [dout("kvs%d" % g, [2, NSQ, WB[g], 2, 8, 128]) for g in range(3)]
    hp_out = dout("hp_out", [2, 128, 10])
    hs_out = dout("hs_out", [2, 128, 10, 4])
    cp_out = dout("cp_out", [2, 10, 128, 3])
    cs_out = dout("cs_out", [2, 10, 128, 4, 3])
    QTs = dscr("QTs", [3, 8, 128, NT])
    KTs = dscr("KTs", [3, 8, 128, NT])
    Vs = dscr("Vs", [3, NT, 1024])
    SKT = [dscr("SKT%d" % g, [8, NSQ * NRES[g], 128, 128]) for g in range(3)]
    SV = [dscr("SV%d" % g, [8, NSQ * NRES[g], 128, 128]) for g in range(3)]
    BTs = dscr("BTs", [3, 8, 128, 256], F32)
    hsend = [dscr("hsend%d" % l, [NHB * 16 * 128, 128]) for l in range(2)]
    hall = [nc.dram_tensor("hall%d" % l, [NCORE * NHB * 16 * 128, 128], BF16, kind="Internal", addr_space="Local").ap() for l in range(2)]
    xsend = [dscr("xsend%d" % l, [128, 24], F32) for l in range(2)]
    xall = [nc.dram_tensor("xall%d" % l, [NCORE * 128, 24], F32, kind="Internal", addr_space="Local").ap() for l in range(2)]
    asend = [dscr("asend%d" % l, [128, 20], F32) for l in range(2)]
    aall = [nc.dram_tensor("aall%d" % l, [NCORE * 128, 20], F32, kind="Internal", addr_space="Local").ap() for l in range(2)]
    RG = dscr("RG", [10, 128, LX])
    RC = dscr("RC", [10, 128, LX])

    es = contextlib.ExitStack()
    with es:
        es.enter_context(nc.allow_non_contiguous_dma(reason="small strided state/aggregate transfers"))
        AW = 47616
        arena_t = es.enter_context(nc.sbuf_tensor("arena", [128, AW], F32))
        ar = Arena(arena_t[:, :], AW)
        ps = [es.enter_context(nc.psum_tensor("ps%d" % i, [128, 512], F32)) for i in range(8)]
        S = Sched(nc, es)
        ccsem = es.enter_context(nc.semaphore("ccsem"))
        ccn = [0]
        ALLG = [list(range(NCORE))]

        xT = ar.f32(8 * NX).rearrange("p (c t) -> p c t", c=8)
        ident = ar.bf16(128)
        ones = ar.bf16(128)
        mf = ar.f32(12)
        mi = ar.i32(2)
        persist_top = ar.top
        negf, notfirst, zero1, epsc = mf[:, 0:1], mf[:, 1:2], mf[:, 10:11], mf[:, 11:12]

        S.dma('sp', lambda e: e.dma_start(out=xT[:, :, 0:NT], in_=xT_in), writes=['xT'])
        S.dma('sp', lambda e: e.dma_start(out=mf, in_=meta_f), writes=['mf'])
        S.dma('sp', lambda e: e.dma_start(out=mi, in_=meta_i), writes=['mi'])
        S.op('pool', lambda e: e.memset(xT[:, :, NT:NX], 0.0), writes=['xTh'])
        S.op('pool', lambda e: e.memset(ones, 1.0), writes=['ones'])
        idf = ar.f32(128)
        S.op('pool', lambda e: e.iota(idf, pattern=[[1, 128]], base=0, channel_multiplier=-1, allow_small_or_imprecise_dtypes=True), writes=['idf'])
        S.op('dve', lambda e: e.tensor_scalar(out=ident, in0=idf, scalar1=0.0, scalar2=0.0, op0=ALU.is_equal, op1=ALU.add), reads=['idf'], writes=['ident'])
        m0 = ar.top
        tabx = ar.f32(24)
        ohr = ar.f32(3 * 384).rearrange("p (g i) -> p g i", g=3)
        bttmp = ar.f32(8 * 256).rearrange("p (h w a) -> p h w a", h=8, w=2)
        S.dma('sp', lambda e: e.dma_start(out=tabx[0:32, :], in_=relb), writes=['tabx'])
        S.dma('sp', lambda e: e.dma_start(out=ohr[0:33], in_=ohr_in), writes=['ohr'])
        S.op('act', lambda e: e.activation(out=tabx[0:32, :], in_=tabx[0:32, :], func=AF.Copy, scale=1.0 / SCALE), reads=['tabx'], writes=['tabx'])
        S.dma('sp', lambda e: e.dma_start(out=tabx[32:33, :], in_=negrow), writes=['tabx2'])
        for g in range(3):
            for w in range(2):
                for half in range(2):
                    pt = ps[half]

                    def mm(e, g=g, w=w, half=half, pt=pt):
                        last = None
                        for al in range(64):
                            a = half * 64 + al
                            i0 = (256 - a) if w == 1 else (128 - a)
                            last = e.matmul(pt[:, al * 8:al * 8 + 8], lhsT=ohr[0:33, g, i0:i0 + 128], rhs=tabx[0:33, g * 8:g * 8 + 8], start=True, stop=True)
                        return last
                    S.op('pe', mm, reads=['ohr', 'tabx', 'tabx2'], writes=[('ps', half)])
                    S.op('act', lambda e, w=w, half=half, pt=pt: e.activation(
                        out=bttmp[:, :, w, half * 64:half * 64 + 64], in_=pt[:, :].rearrange("p (a h) -> p h a", h=8), func=AF.Copy),
                        reads=[('ps', half)], writes=['bttmp'])
            S.dma('pool', lambda e, g=g: e.dma_start(out=BTs[g].rearrange("h k x -> k h x"), in_=bttmp.rearrange("p h w a -> p h (w a)")), reads=['bttmp'], writes=['BTs'])
        S.barrier()
        ar.top = m0

        wrot = [0]

        def load_w(dst, src, kc, ncols, key):
            cpp = max(1, 1024 // ncols)
            c = 0
            while c < kc:
                n = min(cpp, kc - c)
                sidx = wrot[0] % 2
                wrot[0] += 1
                st = wstage[sidx][:, 0:n * ncols].rearrange("p (c n) -> p c n", c=n)
                S.dma('sp', lambda e, st=st, c=c, n=n: e.dma_start(out=st, in_=src[c * 128:(c + n) * 128, :].rearrange("(c p) n -> p c n", p=128)), writes=[('wst', sidx)])
                S.op('pool', lambda e, st=st, c=c, n=n: e.tensor_copy(out=dst[:, c:c + n, :], in_=st), reads=[('wst', sidx)], writes=[key])
                c += n

        def rmsnorm(gam):
            for ti, (t0, n) in enumerate(TILES):
                pt = ps[ti % 2]
                S.op('act', lambda e, t0=t0, n=n: e.activation(out=sqb[:, :, 0:n], in_=xT[:, :, t0:t0 + n], func=AF.Square), reads=['xT', 'xTh'], writes=['sqb'])

                def mm(e, n=n, pt=pt):
                    last = None
                    for c in range(8):
                        last = e.matmul(pt[:, 0:n], lhsT=ones.to_broadcast([128, 128]) if False else onesq[:, :], rhs=sqb[:, c, 0:n], start=(c == 0), stop=(c == 7))
                    return last
                S.op('pe', mm, reads=['sqb', 'onesq'], writes=[('ps', ti % 2)])
                S.op('act', lambda e, n=n, pt=pt: e.activation(out=rstd[:, 0:n], in_=pt[:, 0:n], func=AF.Sqrt, bias=epsc, scale=1.0 / 1024), reads=[('ps', ti % 2), 'mf'], writes=['rstd'])
                S.op('dve', lambda e, n=n: e.reciprocal(out=rstd[:, 0:n], in_=rstd[:, 0:n]), reads=['rstd'], writes=['rstd'])
                for c in range(8):
                    S.op('dve', lambda e, c=c, t0=t0, n=n: e.scalar_tensor_tensor(out=xn[:, c, t0:t0 + n], in0=xT[:, c, t0:t0 + n], scalar=gam[:, c:c + 1], in1=rstd[:, 0:n], op0=ALU.mult, op1=ALU.mult), reads=['xT', 'xTh', 'rstd', 'gam'], writes=['xn'])

        ar.top = persist_top
        onesq = ar.bf16(128)
        identq = ident
        S.op('pool', lambda e: e.memset(onesq, 1.0), writes=['onesq'])
        wstage = [ar.f32(1024), ar.f32(1024)]
        persist_top = ar.top

        def attn_layer(li):
            ar.top = persist_top
            nonlocal_names = {}
            global_refs = {}
            G = ar.bf16(8 * NT).rearrange("p (h t) -> p h t", h=8)
            PB0 = ar.top
            xn_ = ar.bf16(8 * NX).rearrange("p (c t) -> p c t", c=8)
            PB1 = ar.top
            sqb_ = ar.bf16(8 * 512).rearrange("p (c t) -> p c t", c=8)
            rstd_ = ar.f32(512)
            gam = ar.f32(8)
            gq = ar.f32(128)
            gk = ar.f32(128)
            Wb = [ar.bf16(8 * 512).rearrange("p (c n) -> p c n", c=8) for _ in range(2)]
            kf = ar.f32(512)
            sq = ar.f32(512)
            ssq = ar.f32(4)
            knb = ar.bf16(512)
            ktb = ar.bf16(512)
            P1 = ar.top
            return G, PB0, xn_, PB1, sqb_, rstd_, gam, gq, gk, Wb, kf, sq, ssq, knb, ktb, P1

        for layer in range(4):
            li = layer // 2
            if layer % 2 == 0:
                (G, PB0, xn, PB1, sqb, rstd, gam, gq, gk, Wb, kf, sq, ssq, knb, ktb, P1) = attn_layer(li)
                S.dma('sp', lambda e: e.dma_start(out=gam, in_=a_norm[li]), writes=['gam'])
                S.dma('sp', lambda e: e.dma_start(out=gq, in_=a_qn[li]), writes=['gq'])
                S.dma('sp', lambda e: e.dma_start(out=gk, in_=a_kn[li]), writes=['gk'])
                rmsnorm(gam)
                for half in range(2):
                    W = Wb[half]
                    load_w(W, a_win[li][:, 9216 + half * 512:9216 + half * 512 + 512], 8, 512, ('W', half))
                    for hh in range(4):
                        h = half * 4 + hh
                        for ti, (t0, n) in enumerate(TILES):
                            n = min(n, NT - t0)
                            pi = 2 + (ti % 2)

                            def mm(e, W=W, hh=hh, t0=t0, n=n, pi=pi):
                                last = None
                                for c in range(8):
                                    last = e.matmul(ps[pi][:, 0:n], lhsT=W[:, c, hh * 128:hh * 128 + 128], rhs=xn[:, c, t0:t0 + n], start=(c == 0), stop=(c == 7))
                                return last
                            S.op('pe', mm, reads=[('W', half), 'xn'], writes=[('ps', pi)])
                            S.op('act', lambda e, h=h, t0=t0, n=n, pi=pi: e.activation(out=G[:, h, t0:t0 + n], in_=ps[pi][:, 0:n], func=AF.Silu), reads=[('ps', pi)], writes=['G'])
                kf2 = [kf, ar.f32(512)]
                sq2 = [sq, ar.f32(512)]
                ssq2 = [ssq, ar.f32(4)]
                knb2 = [knb, ar.bf16(512)]
                ktb2 = [ktb, ar.bf16(512)]
                units1 = [(g, typ, half) for g in range(3) for typ in range(3) for half in range(2)]

                def w_src(g, typ, half):
                    colbase = (3072, 6144, 0)[typ] + g * 1024
                    return a_win[li][:, colbase + half * 512:colbase + half * 512 + 512]
                load_w(Wb[0], w_src(*units1[0]), 8, 512, ('W', 0))
                it = 0
                for ui1, (g, typ, half) in enumerate(units1):
                    d = DIL[g]
                    nb = 16 // d
                    blocks = []
                    for bi in range(16):
                        r, n_ = divmod(bi, nb)
                        c0 = r + n_ * 128 * d
                        blocks.append((bi, slice(c0, c0 + 127 * d + 1, d), 128))
                    blocks.append((16, slice(NP, NT), 32))
                    W = Wb[ui1 % 2]
                    wkey = ('W', ui1 % 2)
                    if ui1 + 1 < len(units1):
                        load_w(Wb[(ui1 + 1) % 2], w_src(*units1[ui1 + 1]), 8, 512, ('W', (ui1 + 1) % 2))
                    for (bi, cs, M) in blocks:
                        b2 = it % 2
                        it += 1
                        pi = b2
                        pt = ps[pi]
                        kf_, sq_, ssq_, knb_, ktb_ = kf2[b2], sq2[b2], ssq2[b2], knb2[b2], ktb2[b2]
                        kfk, sqk, ssqk, knbk, ktbk = ('kf', b2), ('sq', b2), ('ssq', b2), ('knb', b2), ('ktb', b2)

                        def mm(e, W=W, cs=cs, M=M, pt=pt):
                            last = None
                            for c in range(8):
                                last = e.matmul(pt[0:M, :], lhsT=xn[:, c, cs], rhs=W[:, c, :], start=(c == 0), stop=(c == 7))
                            return last
                        S.op('pe', mm, reads=[wkey, 'xn'], writes=[('ps', pi)])
                        tcol = bi * 128
                        is_halo = (bi < 16) and ((bi % nb) == nb - 1)
                        need_f32 = is_halo or bi == 16
                        hb = HB0[g] + (bi // nb)
                        if typ == 1:
                            if need_f32:
                                S.op('act', lambda e, M=M, pt=pt, kf_=kf_: e.activation(out=kf_[0:M, :], in_=pt[0:M, :], func=AF.Copy), reads=[('ps', pi)], writes=[kfk])
                                if is_halo:
                                    r = bi // nb
                                    S.dma('sp', lambda e, r=r, d=d, kf_=kf_, g=g, half=half: e.dma_start(
                                        out=kvp[g][li, r:r + 127 * d + 1:d, 1, half * 4:half * 4 + 4, :], in_=kf_[:, :].rearrange("p (h d) -> p h d", h=4)), reads=[kfk], writes=[('kvp', g, bi, typ, half)])
                                else:
                                    for s_ in range(NSQ):
                                        S.dma('sp', lambda e, s_=s_, kf_=kf_, g=g, half=half: e.dma_start(
                                            out=kvs[g][li, s_, WB[g] - 8:WB[g], 1, half * 4:half * 4 + 4, :], in_=kf_[s_ * 8:s_ * 8 + 8, :].rearrange("p (h d) -> p h d", h=4)), reads=[kfk], writes=[('kvs', g, s_, typ, half)])
                            S.op('act', lambda e, M=M, pt=pt, knb_=knb_: e.activation(out=knb_[0:M, :], in_=pt[0:M, :], func=AF.Copy), reads=[('ps', pi)], writes=[knbk])
                            S.dma('act', lambda e, M=M, tcol=tcol, knb_=knb_, g=g, half=half: e.dma_start(out=Vs[g, tcol:tcol + M, half * 512:half * 512 + 512], in_=knb_[0:M, :]), reads=[knbk], writes=[('Vs', g, bi, half)])
                            if is_halo:
                                for hh in range(4):
                                    row = ((hb * 8 + half * 4 + hh) * 2 + 1) * 128
                                    S.dma('act', lambda e, hh=hh, row=row, knb_=knb_: e.dma_start(out=hsend[li][row:row + 128, :], in_=knb_[:, hh * 128:hh * 128 + 128]), reads=[knbk], writes=[('hs', row)])
                            continue
                        gvec = gk if typ == 0 else gq
                        v4 = lambda t, M=M: t[0:M, :].rearrange("p (h d) -> p h d", h=4)
                        S.op('act', lambda e, M=M, pt=pt, kf_=kf_: e.activation(out=kf_[0:M, :], in_=pt[0:M, :], func=AF.Copy), reads=[('ps', pi)], writes=[kfk])
                        S.op('dve', lambda e, M=M, kf_=kf_, sq_=sq_: e.tensor_tensor(out=sq_[0:M, :], in0=kf_[0:M, :], in1=kf_[0:M, :], op=ALU.mult), reads=[kfk], writes=[sqk])
                        S.op('dve', lambda e, M=M, sq_=sq_, ssq_=ssq_: e.tensor_reduce(out=ssq_[0:M, :], in_=sq_[0:M, :].rearrange("p (h d) -> p h d", h=4), axis=AX.X, op=ALU.add), reads=[sqk], writes=[ssqk])
                        S.op('act', lambda e, M=M, ssq_=ssq_: e.activation(out=ssq_[0:M, :], in_=ssq_[0:M, :], func=AF.Sqrt, bias=epsc[0:M, :], scale=1.0 / 128), reads=[ssqk, 'mf'], writes=[ssqk])
                        S.op('dve', lambda e, M=M, ssq_=ssq_: e.reciprocal(out=ssq_[0:M, :], in_=ssq_[0:M, :]), reads=[ssqk], writes=[ssqk])
                        S.op('dve', lambda e, M=M, kf_=kf_, ssq_=ssq_, v4=v4: e.tensor_tensor(out=v4(kf_), in0=v4(kf_), in1=ssq_[0:M, :].unsqueeze(2).to_broadcast([M, 4, 128]), op=ALU.mult), reads=[kfk, ssqk], writes=[kfk])
                        if typ == 0 and need_f32:
                            S.op('dve', lambda e, M=M, kf_=kf_, gvec=gvec, v4=v4: e.tensor_tensor(out=v4(kf_), in0=v4(kf_), in1=gvec[0:M, :].unsqueeze(1).to_broadcast([M, 4, 128]), op=ALU.mult), reads=[kfk, 'gq', 'gk'], writes=[kfk])
                            if is_halo:
                                r = bi // nb
                                S.dma('sp', lambda e, r=r, d=d, kf_=kf_, g=g, half=half: e.dma_start(
                                    out=kvp[g][li, r:r + 127 * d + 1:d, 0, half * 4:half * 4 + 4, :], in_=kf_[:, :].rearrange("p (h d) -> p h d", h=4)), reads=[kfk], writes=[('kvp', g, bi, typ, half)])
                            else:
                                for s_ in range(NSQ):
                                    S.dma('sp', lambda e, s_=s_, kf_=kf_, g=g, half=half: e.dma_start(
                                        out=kvs[g][li, s_, WB[g] - 8:WB[g], 0, half * 4:half * 4 + 4, :], in_=kf_[s_ * 8:s_ * 8 + 8, :].rearrange("p (h d) -> p h d", h=4)), reads=[kfk], writes=[('kvs', g, s_, typ, half)])
                            S.op('act', lambda e, M=M, kf_=kf_, knb_=knb_: e.activation(out=knb_[0:M, :], in_=kf_[0:M, :], func=AF.Copy), reads=[kfk], writes=[knbk])
                        else:
                            S.op('dve', lambda e, M=M, kf_=kf_, knb_=knb_, gvec=gvec, v4=v4: e.tensor_tensor(out=v4(knb_), in0=v4(kf_), in1=gvec[0:M, :].unsqueeze(1).to_broadcast([M, 4, 128]), op=ALU.mult), reads=[kfk, 'gq', 'gk'], writes=[knbk])
                        ptb = ps[4 + pi][:, :].bitcast(BF16)

                        def tr(e, M=M, ptb=ptb, knb_=knb_):
                            last = None
                            for hh in range(4):
                                last = e.transpose(ptb[:, hh * 128:hh * 128 + M], knb_[0:M, hh * 128:hh * 128 + 128], ident[0:M, 0:M])
                            return last
                        S.op('pe', tr, reads=[knbk, 'ident'], writes=[('ps', 4 + pi)])
                        S.op('act', lambda e, M=M, ptb=ptb, ktb_=ktb_: e.activation(out=ktb_[:, :].rearrange("p (h t) -> p h t", h=4)[:, :, 0:M], in_=ptb[:, 0:512].rearrange("p (h t) -> p h t", h=4)[:, :, 0:M], func=AF.Copy),
                             reads=[('ps', 4 + pi)], writes=[ktbk])
                        dst = KTs if typ == 0 else QTs
                        S.dma('act', lambda e, M=M, tcol=tcol, dst=dst, ktb_=ktb_, g=g, half=half: e.dma_start(
                            out=dst[g, half * 4:half * 4 + 4, :, tcol:tcol + M].rearrange("h d t -> d h t"), in_=ktb_[:, :].rearrange("p (h t) -> p h t", h=4)[:, :, 0:M]),
                            reads=[ktbk], writes=[('QK', typ, g, bi, half)])
                        if typ == 0 and is_halo:
                            for hh in range(4):
                                row = ((hb * 8 + half * 4 + hh) * 2 + 0) * 128
                                S.dma('act', lambda e, hh=hh, row=row, ktb_=ktb_: e.dma_start(out=hsend[li][row:row + 128, :], in_=ktb_[:, hh * 128:hh * 128 + 128]), reads=[ktbk], writes=[('hs', row)])
                S.barrier()
                S.op('pool', lambda e: e.collective_compute("AllGather", ALU.bypass, replica_groups=ALLG, ins=[hsend[li]], outs=[hall[li]]), writes=['hall'])
                if li == 0:
                    for g in range(3):
                        for l2 in range(2):
                            for s_ in range(NSQ):
                                n = WB[g] - 8
                                r0 = 0
                                while r0 < n:
                                    nr = min(512, n - r0)
                                    nc.scalar.dma_start(out=kvs[g][l2, s_, r0:r0 + nr].rearrange("r a h d -> r (a h d)"),
                                                        in_=cache[g][l2, s_, 8 + r0:8 + r0 + nr].rearrange("r a h d -> r (a h d)")).then_inc(ccsem, 16)
                                    ccn[0] += 16
                                    r0 += nr
                ar.top = PB0
                acc = ar.f32(2 * NT).rearrange("p (w t) -> p w t", w=2)
                accS = ar.f32(8 * 2 * 32).rearrange("p (h w t) -> p h w t", h=8, w=2)
                NB = 4
                stt = [ar.f32(256).rearrange("p (w a) -> p w a", w=2) for _ in range(NB)]
                pT = [ar.bf16(256).rearrange("p (w a) -> p w a", w=2) for _ in range(NB)]
                P2 = ar.top
                ck = [ar.f32(2048) for _ in range(2)]
                ckb = [ar.bf16(2048) for _ in range(2)]
                ckt = [ar.bf16(1024).rearrange("p (h k) -> p h k", h=8) for _ in range(2)]
                vcs = [ar.bf16(1024) for _ in range(2)]
                QS = ar.bf16(3 * 8 * 32).rearrange("p (g h t) -> p g h t", g=3, h=8)
                KS = ar.bf16(3 * 8 * 32).rearrange("p (g h t) -> p g h t", g=3, h=8)
                BTa = ar.f32(3 * 8 * 16).rearrange("p (g h w a) -> p g h w a", g=3, h=8, w=2)
                LAG = 3
                for g in range(3):
                    for hh8 in range(0, 8, 4):
                        S.dma('sp', lambda e, g=g, hh8=hh8: e.dma_start(out=QS[:, g, hh8:hh8 + 4, :], in_=QTs[g, hh8:hh8 + 4, :, NP:NT].rearrange("h d t -> d h t")), writes=['QS'])
                        S.dma('sp', lambda e, g=g, hh8=hh8: e.dma_start(out=KS[:, g, hh8:hh8 + 4, :], in_=KTs[g, hh8:hh8 + 4, :, NP:NT].rearrange("h d t -> d h t")), writes=['KS'])
                    for w_ in range(2):
                        S.dma('sp', lambda e, g=g, w_=w_: e.dma_start(out=BTa[:, g, :, w_, :], in_=BTs[g][:, :, w_ * 128:w_ * 128 + 8].rearrange("h k a -> k h a")), writes=['BTa'])
                S.op('dve', lambda e: e.memset(accS, 0.0), writes=['accS'])
                pipe = []
                cnt2 = [0]

                def stageA(job):
                    i = job['i']
                    b4 = i % NB
                    pss = ps[b4]
                    st_, pT_ = stt[b4], pT[b4]
                    M, nkc = job['M'], job['nkc']
                    S.op('pe', lambda e: (e.matmul(pss[:, 0:M], lhsT=job['kprev'], rhs=job['q'], start=True, stop=True),
                                          e.matmul(pss[0:nkc, 128:128 + M], lhsT=job['kcur'], rhs=job['q'], start=True, stop=True))[1],
                         reads=job['rd1'], writes=[('ps', b4)])
                    bt = job['bt']
                    if job['merge']:
                        S.op('dve', lambda e: e.tensor_tensor(out=st_[:, :, :], in0=pss[:, 0:256].rearrange("p (w a) -> p w a", w=2), in1=bt[:, :, :], op=ALU.add), reads=[('ps', b4)] + job['rdbt'], writes=[('st', b4), ('st2', b4)])
                        S.op('act', lambda e: e.activation(out=pT_[:, :, :], in_=st_[:, :, :], func=AF.Exp, scale=SCALE), reads=[('st', b4), ('st2', b4)], writes=[('pT', b4), ('pT2', b4)])
                    else:
                        S.op('dve', lambda e: e.tensor_tensor(out=st_[:, 0, 0:M], in0=pss[:, 0:M], in1=bt[:, 0, 0:M], op=ALU.add), reads=[('ps', b4)] + job['rdbt'], writes=[('st', b4)])
                        S.op('dve', lambda e: e.tensor_tensor(out=st_[0:nkc, 1, 0:M], in0=pss[0:nkc, 128:128 + M], in1=bt[0:nkc, 1, 0:M], op=ALU.add), reads=[('ps', b4)] + job['rdbt'], writes=[('st2', b4)])
                        S.op('act', lambda e: e.activation(out=pT_[:, 0, 0:M], in_=st_[:, 0, 0:M], func=AF.Exp, bias=job['bprev'], scale=SCALE), reads=[('st', b4), 'mf'], writes=[('pT', b4)])
                        S.op('act', lambda e: e.activation(out=pT_[0:nkc, 1, 0:M], in_=st_[0:nkc, 1, 0:M], func=AF.Exp, scale=SCALE), reads=[('st2', b4)], writes=[('pT2', b4)])

                def stageB(job):
                    i = job['i']
                    b4 = i % NB
                    pso = ps[4 + b4]
                    pT_ = pT[b4]
                    M, nkc = job['M'], job['nkc']

                    def mm2(e):
                        e.matmul(pso[:, 0:M], lhsT=job['vprev'], rhs=pT_[:, 0, 0:M], start=True, stop=False)
                        e.matmul(pso[:, 0:M], lhsT=job['vcur'], rhs=pT_[0:nkc, 1, 0:M], start=False, stop=True)
                        e.matmul(pso[:, 128:128 + M], lhsT=onesq[:, :], rhs=pT_[:, 0, 0:M], start=True, stop=False)
                        return e.matmul(pso[:, 128:128 + M], lhsT=onesq[0:nkc, :], rhs=pT_[0:nkc, 1, 0:M], start=False, stop=True)
                    S.op('pe', mm2, reads=[('pT', b4), ('pT2', b4), 'onesq'] + job['rd2'], writes=[('ps', 4 + b4)])
                    src = pso[:, 0:256].rearrange("p (w a) -> p w a", w=2)[:, :, 0:M]
                    if job['first']:
                        S.op('act', lambda e: e.activation(out=job['dst'], in_=src, func=AF.Copy), reads=[('ps', 4 + b4)], writes=[job['dkey']])
                    else:
                        S.op('dve', lambda e: e.tensor_tensor(out=job['dst'], in0=job['dst'], in1=src, op=ALU.add), reads=[('ps', 4 + b4), job['dkey']], writes=[job['dkey']])

                def submit(job):
                    job['i'] = cnt2[0]
                    cnt2[0] += 1
                    stageA(job)
                    pipe.append(job)
                    if len(pipe) > LAG:
                        stageB(pipe.pop(0))

                def flush():
                    while pipe:
                        stageB(pipe.pop(0))

                ui = 0
                for g in range(3):
                    d = DIL[g]
                    nres, nq = NRES[g], NQ[g]
                    for s_ in range(NSQ):
                        for r in range(nres):
                            u2 = ui % 2
                            ui += 1
                            cb, cbb, ct, vc = ck[u2], ckb[u2], ckt[u2], vcs[u2]
                            S.dma('sp', lambda e, g=g, s_=s_, r=r, d=d, cb=cb: e.dma_start(out=cb, in_=cache[g][li, s_, r:r + 127 * d + 1:d].rearrange("r a h d -> r (a h d)")), writes=[('ck', u2)])
                            S.dma('sp', lambda e, g=g, s_=s_, r=r, d=d, vc=vc, nq=nq: e.dma_start(out=vc[0:nq, :], in_=Vs[g, NP + s_ * 8 + r:NP + s_ * 8 + r + (nq - 1) * d + 1:d, :]), writes=[('vc', u2)])
                            S.op('act', lambda e, cb=cb, cbb=cbb: e.activation(out=cbb[:, 0:1024], in_=cb[:, 0:1024], func=AF.Copy), reads=[('ck', u2)], writes=[('ckbk', u2)])
                            S.op('dve', lambda e, cb=cb, cbb=cbb: e.tensor_copy(out=cbb[:, 1024:2048], in_=cb[:, 1024:2048]), reads=[('ck', u2)], writes=[('ckbv', u2)])
                            for hf in range(2):
                                ptb = ps[hf][:, :].bitcast(BF16)

                                def tr(e, hf=hf, ptb=ptb, cbb=cbb):
                                    last = None
                                    for hh in range(4):
                                        last = e.transpose(ptb[:, hh * 128:hh * 128 + 128], cbb[:, (hf * 4 + hh) * 128:(hf * 4 + hh) * 128 + 128], ident[:, :])
                                    return last
                                S.op('pe', tr, reads=[('ckbk', u2), 'ident'], writes=[('ps', hf)])
                                S.op('act', lambda e, hf=hf, ptb=ptb, ct=ct: e.activation(out=ct[:, hf * 4:hf * 4 + 4, :], in_=ptb[:, 0:512].rearrange("p (h k) -> p h k", h=4), func=AF.Copy), reads=[('ps', hf)], writes=[('ckt', u2, hf)])
                            qc = slice(s_ * 8 + r, s_ * 8 + r + (nq - 1) * d + 1, d)
                            for h in range(8):
                                submit(dict(M=nq, nkc=nq, kprev=ct[:, h, :], kcur=KS[:, g, h, qc], q=QS[:, g, h, qc], bt=BTa[:, g, h], merge=False, bprev=zero1,
                                            vprev=cbb[:, 1024 + h * 128:1024 + h * 128 + 128], vcur=vc[0:nq, h * 128:h * 128 + 128],
                                            rd1=[('ckt', u2, 0), ('ckt', u2, 1), 'QS', 'KS'], rdbt=['BTa'], rd2=[('ckbv', u2), ('vc', u2)],
                                            first=False, dst=accS[:, h, :, qc], dkey='accS'))
                flush()
                S.barrier()
                ar.top = P2
                LD = []
                for _ in range(2):
                    LD.append(dict(Q=ar.bf16(NP), K=ar.bf16(NP), V=ar.bf16(16 * 128).rearrange("p (b d) -> p b d", b=16),
                                   KH=ar.bf16(16 * 128).rearrange("p (b k) -> p b k", b=16), VH=ar.bf16(16 * 128).rearrange("p (b d) -> p b d", b=16),
                                   BT=ar.f32(256).rearrange("p (w a) -> p w a", w=2)))
                hg = [(h, g) for h in range(8) for g in range(3)]

                def loads(k):
                    h, g = hg[k]
                    L = LD[k % 2]
                    lk = k % 2
                    S.dma('sp', lambda e: e.dma_start(out=L['Q'], in_=QTs[g, h, :, 0:NP]), writes=[('LQ', lk)])
                    S.dma('sp', lambda e: e.dma_start(out=L['K'], in_=KTs[g, h, :, 0:NP]), writes=[('LK', lk)])
                    S.dma('sp', lambda e: e.dma_start(out=L['V'], in_=Vs[g, 0:NP, h * 128:h * 128 + 128].rearrange("(b k) d -> k b d", k=128)), writes=[('LV', lk)])
                    S.dma('sp', lambda e: e.dma_start(out=L['BT'], in_=BTs[g, h].rearrange("k (w a) -> k w a", w=2)), writes=[('LBT', lk)])
                    for hb in range(HPB[g]):
                        for kv in range(2):
                            eo = (((HB0[g] + hb) * 8 + h) * 2 + kv) * 128 * 128
                            dst = (L['KH'] if kv == 0 else L['VH'])[:, hb, :]
                            S.dma('pool', lambda e, dst=dst, eo=eo: e.indirect_dma_start(out=dst, out_offset=None, in_=hall[li], in_offset=bass.IndirectOffsetOnAxis(ap=mi[:, 0:1], axis=0), element_offset=eo),
                                  reads=['hall', 'mi'], writes=[('LKH', lk) if kv == 0 else ('LVH', lk)])
                loads(0)
                for k, (h, g) in enumerate(hg):
                    L = LD[k % 2]
                    lk = k % 2
                    d = DIL[g]
                    nb = 16 // d
                    for bi in range(16):
                        if bi == LAG + 1 and k + 1 < len(hg):
                            loads(k + 1)
                        r, n_ = divmod(bi, nb)
                        qc = slice(bi * 128, bi * 128 + 128)
                        c0 = r + n_ * 128 * d
                        dest = slice(c0, c0 + 127 * d + 1, d)
                        if n_ == 0:
                            kprev, vprev, bprev, merge = L['KH'][:, r, :], L['VH'][:, r, :], negf, False
                        else:
                            kprev, vprev, bprev, merge = L['K'][:, (bi - 1) * 128:bi * 128], L['V'][:, bi - 1, :], zero1, True
                        submit(dict(M=128, nkc=128, kprev=kprev, kcur=L['K'][:, qc], q=L['Q'][:, qc], bt=L['BT'], merge=merge, bprev=bprev,
                                    vprev=vprev, vcur=L['V'][:, bi, :], rd1=[('LQ', lk), ('LK', lk), ('LKH', lk)], rdbt=[('LBT', lk)], rd2=[('LV', lk), ('LVH', lk)],
                                    first=(g == 0), dst=acc[:, :, dest], dkey='acc'))
                    if g == 2:
                        flush()
                        S.op('dve', lambda e, h=h: e.tensor_copy(out=acc[:, :, NP:NT], in_=accS[:, h, :, :]), reads=['accS', 'acc'], writes=['acc'])
                        S.op('dve', lambda e: e.reciprocal(out=acc[:, 1, :], in_=acc[:, 1, :]), reads=['acc'], writes=['acc'])
                        S.op('dve', lambda e: e.tensor_tensor(out=acc[:, 0, :], in0=acc[:, 0, :], in1=acc[:, 1, :], op=ALU.mult), reads=['acc'], writes=['acc'])
                        S.op('dve', lambda e, h=h: e.tensor_tensor(out=G[:, h, :], in0=G[:, h, :], in1=acc[:, 0, :], op=ALU.mult), reads=['acc', 'G'], writes=['G'])
                S.barrier()
                Wo = LD[0]['Q'][:, 0:2048].rearrange("p (c n) -> p c n", c=8)
                Wo2 = LD[1]['Q'][:, 0:2048].rearrange("p (c n) -> p c n", c=8)
                Wos = [Wo, Wo2]
                load_w(Wos[0], a_wout[li][:, 0:256], 8, 256, ('Wo', 0))
                for q4 in range(4):
                    if q4 + 1 < 4:
                        load_w(Wos[(q4 + 1) % 2], a_wout[li][:, (q4 + 1) * 256:(q4 + 2) * 256], 8, 256, ('Wo', (q4 + 1) % 2))
                    Wq = Wos[q4 % 2]
                    for cc in range(2):
                        dmc = q4 * 2 + cc
                        for ti, (t0, n) in enumerate(TILES):
                            n = min(n, NT - t0)
                            pi = (dmc * 5 + ti) % 4

                            def mm(e, cc=cc, t0=t0, n=n, pi=pi, Wq=Wq):
                                last = None
                                for hh in range(8):
                                    last = e.matmul(ps[pi][:, 0:n], lhsT=Wq[:, hh, cc * 128:cc * 128 + 128], rhs=G[:, hh, t0:t0 + n], start=(hh == 0), stop=(hh == 7))
                                return last
                            S.op('pe', mm, reads=[('Wo', q4 % 2), 'G'], writes=[('ps', pi)])
                            S.op('dve', lambda e, dmc=dmc, t0=t0, n=n, pi=pi: e.tensor_tensor(out=xT[:, dmc, t0:t0 + n], in0=xT[:, dmc, t0:t0 + n], in1=ps[pi][:, 0:n], op=ALU.add), reads=[('ps', pi), 'xT'], writes=['xT'])
                S.barrier()
            else:
                ar.top = persist_top
                xn = ar.bf16(8 * NX).rearrange("p (c t) -> p c t", c=8)
                gam = ar.f32(8)
                cw = ar.f32(40).rearrange("p (j k) -> p j k", j=10)
                cb_ = ar.f32(10)
                gab = ar.f32(10)
                gxb = ar.f32(10)
                lamc = ar.f32(10)
                h0s = ar.f32(40).rearrange("p (j s) -> p j s", j=10)
                GA = ar.bf16(1280).rearrange("p (j o) -> p j o", j=10)
                GX = ar.bf16(1280).rearrange("p (j o) -> p j o", j=10)
                Wj = ar.bf16(8 * 256).rearrange("p (c n) -> p c n", c=8)
                AGG = ar.f32(100).rearrange("p (j k) -> p j k", j=10)
                XB = ar.f32(LX)
                XC = ar.f32(LX)
                XCb = ar.bf16(LX)
                RA = ar.f32(LX)
                IB = ar.f32(LX)
                TT = XB
                mk_ = ar.top
                sqb = ar.bf16(8 * 512).rearrange("p (c t) -> p c t", c=8)
                rstd = ar.f32(512)
                ar.top = mk_
                HL = ar.f32(LX)
                AC = ar.f32(LX)
                SG = ar.bf16(LX)
                GP = XCb
                CP = ar.bf16(LX)
                xh = ar.f32(24)
                S.dma('pool', lambda e: e.dma_start(out=xsend[li].rearrange("p (c t) -> p c t", c=8), in_=xT[:, :, NP - 3:NP]), reads=['xT'], writes=['xsend'])
                S.barrier()
                S.op('pool', lambda e: e.collective_compute("AllGather", ALU.bypass, replica_groups=ALLG, ins=[xsend[li]], outs=[xall[li]]), reads=['xsend'], writes=['xall'])
                S.dma('pool', lambda e: e.indirect_dma_start(out=xh, out_offset=None, in_=xall[li], in_offset=bass.IndirectOffsetOnAxis(ap=mi[:, 1:2], axis=0)), reads=['xall', 'mi'], writes=['xh'])
                S.op('dve', lambda e: e.tensor_scalar(out=xT[:, :, NT:NX], in0=xh[:, :].rearrange("p (c t) -> p c t", c=8), scalar1=notfirst, scalar2=0.0, op0=ALU.mult, op1=ALU.add), reads=['xh', 'mf'], writes=['xTh'])
                for (dst, src, key) in ((gam, r_norm[li], 'gam'), (cw, r_cw[li], 'cw'), (cb_, r_cb[li], 'cb'), (gab, r_gab[li], 'gab'), (gxb, r_gxb[li], 'gxb'), (lamc, r_lam[li], 'lamc'), (h0s, h0s_in[li], 'h0s')):
                    S.dma('sp', lambda e, dst=dst, src=src: e.dma_start(out=dst, in_=src), writes=[key])
                S.op('act', lambda e: e.activation(out=lamc, in_=lamc, func=AF.Exp, scale=-1.0), reads=['lamc'], writes=['lamc'])
                S.op('act', lambda e: e.activation(out=lamc, in_=lamc, func=AF.Ln, bias=1.0, scale=1.0), reads=['lamc'], writes=['lamc'])
                S.op('dve', lambda e: e.tensor_scalar(out=lamc, in0=lamc, scalar1=-8.0, scalar2=0.0, op0=ALU.mult, op1=ALU.add), reads=['lamc'], writes=['lamc'])
                for (dstw, srcw, key) in ((GA, r_gaw[li], 'GA'), (GX, r_gxw[li], 'GX')):
                    for j0 in range(0, 10, 5):
                        st = XB[:, 0:640].rearrange("p (j o) -> p j o", j=5)
                        S.dma('sp', lambda e, st=st, srcw=srcw, j0=j0: e.dma_start(out=st, in_=srcw[j0:j0 + 5].rearrange("j i o -> i j o")), writes=['XB'])
                        S.op('pool', lambda e, st=st, dstw=dstw, j0=j0: e.tensor_copy(out=dstw[:, j0:j0 + 5, :], in_=st), reads=['XB'], writes=[key])
                rmsnorm(gam)
                S.barrier()
                S.op('pool', lambda e: e.memset(XC[:, 0:3], 0.0), writes=['XC'])
                SEGS = [slice(3, 3 + NP)] + [slice(NP + 3 + 11 * s + 3, NP + 3 + 11 * s + 11) for s in range(NSQ)]
                sview = lambda t: t[:, NP + 3:LX].rearrange("p (s e) -> p s e", e=11)
                for j in range(10):
                    load_w(Wj[:, :, 0:128], r_win[li][:, j * 128:j * 128 + 128], 8, 128, 'Wj0')
                    load_w(Wj[:, :, 128:256], r_win[li][:, 1280 + j * 128:1280 + j * 128 + 128], 8, 128, 'Wj1')
                    S.dma('sp', lambda e, j=j: e.dma_start(out=sview(XB)[:, :, 0:3], in_=cst_in[li][:, j]), writes=['XB'])
                    for which in range(2):
                        for ti, (t0, n) in enumerate(TILES):
                            pi = ti % 2 + 2 * which

                            def mm(e, which=which, t0=t0, n=n, pi=pi):
                                last = None
                                for c in range(8):
                                    last = e.matmul(ps[pi][:, 0:n], lhsT=Wj[:, c, which * 128:which * 128 + 128], rhs=xn[:, c, t0:t0 + n], start=(c == 0), stop=(c == 7))
                                return last
                            S.op('pe', mm, reads=['Wj0', 'Wj1', 'xn'], writes=[('ps', pi)])
                            dstt = XB if which == 0 else SG
                            func = AF.Copy if which == 0 else AF.Silu
                            dkey = 'XB' if which == 0 else 'SG'
                            if ti < 4:
                                S.op('act', lambda e, dstt=dstt, func=func, t0=t0, n=n, pi=pi: e.activation(out=dstt[:, 3 + t0:3 + t0 + n], in_=ps[pi][:, 0:n], func=func), reads=[('ps', pi)], writes=[dkey])
                            else:
                                S.op('act', lambda e, dstt=dstt, func=func, pi=pi: e.activation(out=sview(dstt)[:, :, 3:11], in_=ps[pi][:, 0:32].rearrange("p (s t) -> p s t", t=8), func=func), reads=[('ps', pi)], writes=[dkey])
                                S.op('act', lambda e, dstt=dstt, func=func, pi=pi: e.activation(out=dstt[:, 0:3], in_=ps[pi][:, 32:35], func=func), reads=[('ps', pi)], writes=[dkey])
                    S.dma('pool', lambda e, j=j: e.dma_start(out=cp_out[li, j], in_=XB[:, NP:NP + 3]), reads=['XB'], writes=[('cp', j)])
                    S.dma('pool', lambda e, j=j: e.dma_start(out=cs_out[li, j], in_=sview(XB)[:, :, 8:11]), reads=['XB'], writes=[('cs', j)])
                    n_ = LX - 3
                    S.op('dve', lambda e, j=j: e.tensor_scalar(out=XC[:, 3:LX], in0=XB[:, 0:n_], scalar1=cw[:, j, 0:1], scalar2=cb_[:, j:j + 1], op0=ALU.mult, op1=ALU.add), reads=['XB', 'cw', 'cb'], writes=['XC'])
                    for k in range(1, 4):
                        S.op('dve', lambda e, j=j, k=k: e.scalar_tensor_tensor(out=XC[:, 3:LX], in0=XB[:, k:k + n_], scalar=cw[:, j, k:k + 1], in1=XC[:, 3:LX], op0=ALU.mult, op1=ALU.add), reads=['XB', 'XC', 'cw'], writes=['XC'])
                    S.op('pool', lambda e: e.tensor_copy(out=XCb, in_=XC), reads=['XC'], writes=['XCb'])
                    for which in range(2):
                        Wg = GA if which == 0 else GX
                        bb = gab if which == 0 else gxb
                        dstt = RA if which == 0 else IB
                        dkey = 'RA' if which == 0 else 'IB'
                        for ti, (t0, n) in enumerate(XTILES):
                            pi = 4 + ti % 2 + 2 * which
                            S.op('pe', lambda e, Wg=Wg, j=j, t0=t0, n=n, pi=pi: e.matmul(ps[pi][:, 0:n], lhsT=Wg[:, j, :], rhs=XCb[:, t0:t0 + n], start=True, stop=True), reads=['XCb', 'GA', 'GX'], writes=[('ps', pi)])
                            S.op('act', lambda e, bb=bb, dstt=dstt, j=j, t0=t0, n=n, pi=pi: e.activation(out=dstt[:, t0:t0 + n], in_=ps[pi][:, 0:n], func=AF.Sigmoid, bias=bb[:, j:j + 1], scale=1.0), reads=[('ps', pi), 'gab', 'gxb'], writes=[dkey])
                    S.op('act', lambda e, j=j: e.activation(out=RA, in_=RA, func=AF.Exp, scale=lamc[:, j:j + 1]), reads=['RA', 'lamc'], writes=['RA'])
                    S.op('dve', lambda e: e.tensor_tensor(out=TT, in0=RA, in1=RA, op=ALU.mult), reads=['RA'], writes=['XB'])
                    S.op('act', lambda e: e.activation(out=TT, in_=TT, func=AF.Sqrt, bias=1.0, scale=-1.0), reads=['XB'], writes=['XB'])
                    S.op('dve', lambda e: e.tensor_tensor(out=IB, in0=IB, in1=TT, op=ALU.mult), reads=['IB', 'XB'], writes=['IB'])
                    S.op('dve', lambda e: e.tensor_tensor(out=IB, in0=IB, in1=XC, op=ALU.mult), reads=['IB', 'XC'], writes=['IB'])
                    S.op('pool', lambda e: e.memset(HL, 0.0), writes=['HL'])
                    S.op('pool', lambda e: e.memset(AC, 0.0), writes=['AC'])
                    for sg in SEGS:
                        S.op('dve', lambda e, sg=sg: e.tensor_tensor_scan(out=HL[:, sg], data0=RA[:, sg], data1=IB[:, sg], initial=0.0, op0=ALU.mult, op1=ALU.add), reads=['RA', 'IB'], writes=['HL'])
                        S.op('dve', lambda e, sg=sg: e.tensor_tensor_scan(out=AC[:, sg], data0=RA[:, sg], data1=RA[:, sg], initial=1.0, op0=ALU.mult, op1=ALU.min), reads=['RA'], writes=['AC'])
                    S.op('act', lambda e, j=j: e.activation(out=AGG[:, j, 0:1], in_=AC[:, NP + 2:NP + 3], func=AF.Copy), reads=['AC'], writes=['AGG'])
                    S.op('act', lambda e, j=j: e.activation(out=AGG[:, j, 1:5], in_=sview(AC)[:, :, 10], func=AF.Copy), reads=['AC'], writes=['AGG'])
                    S.op('act', lambda e, j=j: e.activation(out=AGG[:, j, 5:6], in_=HL[:, NP + 2:NP + 3], func=AF.Copy), reads=['HL'], writes=['AGG'])
                    S.op('act', lambda e, j=j: e.activation(out=AGG[:, j, 6:10], in_=sview(HL)[:, :, 10], func=AF.Copy), reads=['HL'], writes=['AGG'])
                    S.op('dve', lambda e: e.tensor_tensor(out=GP, in0=SG, in1=HL, op=ALU.mult), reads=['SG', 'HL'], writes=['XCb'])
                    S.op('dve', lambda e: e.tensor_tensor(out=CP, in0=SG, in1=AC, op=ALU.mult), reads=['SG', 'AC'], writes=['CP'])
                    S.dma('pool', lambda e, j=j: e.dma_start(out=RG[j], in_=GP), reads=['XCb'], writes=[('RG', j)])
                    S.dma('pool', lambda e, j=j: e.dma_start(out=RC[j], in_=CP), reads=['CP'], writes=[('RC', j)])
                AG8 = XB[:, 0:160].rearrange("p (r f) -> p r f", r=8)
                AE = XC[:, 0:80].rearrange("p (r j) -> p r j", r=8)
                HE = XC[:, 80:160].rearrange("p (r j) -> p r j", r=8)
                stc = RA[:, 0:10]
                HIN = IB[:, 0:50].rearrange("p (j s) -> p j s", j=10)
                HF = TT[:, 200:250].rearrange("p (j s) -> p j s", j=10)
                S.barrier()
                S.dma('pool', lambda e: e.dma_start(out=asend[li].rearrange("p (j k) -> p j k", k=2), in_=AGG[:, :, 0:10:5]), reads=['AGG'], writes=['asend'])
                S.barrier()
                S.op('pool', lambda e: e.collective_compute("AllGather", ALU.bypass, replica_groups=ALLG, ins=[asend[li]], outs=[aall[li]]), reads=['asend'], writes=['aall'])
                S.dma('pool', lambda e: e.dma_start(out=AG8, in_=aall[li].rearrange("(r p) f -> p r f", p=128)), reads=['aall'], writes=['AG8'])
                mr = mf[:, 2:10].unsqueeze(2).to_broadcast([128, 8, 10])
                agA = AG8.rearrange("p r (j k) -> p r j k", k=2)[:, :, :, 0]
                agH = AG8.rearrange("p r (j k) -> p r j k", k=2)[:, :, :, 1]
                S.op('dve', lambda e: e.tensor_scalar(out=AE, in0=agA, scalar1=-1.0, scalar2=0.0, op0=ALU.add, op1=ALU.add), reads=['AG8'], writes=['AE'])
                S.op('dve', lambda e: e.tensor_tensor(out=AE, in0=AE, in1=mr, op=ALU.mult), reads=['AE', 'mf'], writes=['AE'])
                S.op('dve', lambda e: e.tensor_scalar(out=AE, in0=AE, scalar1=1.0, scalar2=0.0, op0=ALU.add, op1=ALU.add), reads=['AE'], writes=['AE'])
                S.op('dve', lambda e: e.tensor_tensor(out=HE, in0=agH, in1=mr, op=ALU.mult), reads=['AG8', 'mf'], writes=['HE'])
                S.op('dve', lambda e: e.memset(stc, 0.0), writes=['stc'])
                for r in range(8):
                    S.op('dve', lambda e, r=r: e.tensor_tensor(out=stc, in0=stc, in1=AE[:, r, :], op=ALU.mult), reads=['stc', 'AE'], writes=['stc'])
                    S.op('dve', lambda e, r=r: e.tensor_tensor(out=stc, in0=stc, in1=HE[:, r, :], op=ALU.add), reads=['stc', 'HE'], writes=['stc'])
                S.op('dve', lambda e: e.tensor_copy(out=HIN[:, :, 0], in_=stc), reads=['stc'], writes=['HIN'])
                S.op('dve', lambda e: e.tensor_copy(out=HIN[:, :, 1:5], in_=h0s), reads=['h0s', 'HIN'], writes=['HIN'])
                S.op('dve', lambda e: e.tensor_tensor(out=HF, in0=AGG[:, :, 0:5], in1=HIN, op=ALU.mult), reads=['AGG', 'HIN'], writes=['HF'])
                S.op('dve', lambda e: e.tensor_tensor(out=HF, in0=HF, in1=AGG[:, :, 5:10], op=ALU.add), reads=['AGG', 'HF'], writes=['HF'])
                S.dma('pool', lambda e: e.dma_start(out=hp_out[li], in_=HF[:, :, 0]), reads=['HF'], writes=['hp'])
                S.dma('pool', lambda e: e.dma_start(out=hs_out[li], in_=HF[:, :, 1:5]), reads=['HF'], writes=['hs'])
                GJ = SG
                GSm = XC[:, 256:272].bitcast(BF16)
                Woj = ar.bf16(1024)
                Wov = Woj[:, :].rearrange("p (c n) -> p c n", c=1)
                for j in range(10):
                    S.dma('sp', lambda e, j=j: e.dma_start(out=GP, in_=RG[j]), writes=['XCb'])
                    S.dma('sp', lambda e, j=j: e.dma_start(out=CP, in_=RC[j]), writes=['CP'])
                    load_w(Wov, r_wout[li][j * 128:j * 128 + 128, :], 1, 1024, 'Woj')
                    for si, sg in enumerate(SEGS):
                        S.op('dve', lambda e, j=j, si=si, sg=sg: e.scalar_tensor_tensor(out=GJ[:, sg], in0=CP[:, sg], scalar=HIN[:, j, si:si + 1], in1=GP[:, sg], op0=ALU.mult, op1=ALU.add), reads=['CP', 'XCb', 'HIN'], writes=['GJ'])
                    S.op('pool', lambda e: e.tensor_copy(out=GSm.rearrange("p (s t) -> p s t", t=8), in_=sview(GJ)[:, :, 3:11]), reads=['GJ'], writes=['GSm'])
                    for dmc in range(8):
                        for ti, (t0, n) in enumerate(TILES):
                            pi = (dmc * 5 + ti) % 4
                            if ti < 4:
                                rhs = GJ[:, 3 + t0:3 + t0 + n]
                            else:
                                n = 32
                                rhs = GSm
                            S.op('pe', lambda e, dmc=dmc, rhs=rhs, n=n, pi=pi: e.matmul(ps[pi][:, 0:n], lhsT=Woj[:, dmc * 128:dmc * 128 + 128], rhs=rhs, start=True, stop=True), reads=['Woj', 'GJ', 'GSm'], writes=[('ps', pi)])
                            S.op('dve', lambda e, dmc=dmc, t0=t0, n=n, pi=pi: e.tensor_tensor(out=xT[:, dmc, t0:t0 + n], in0=xT[:, dmc, t0:t0 + n], in1=ps[pi][:, 0:n], op=ALU.add), reads=[('ps', pi), 'xT'], writes=['xT'])
                S.barrier()
        S.dma('sp', lambda e: e.dma_start(out=yT_out, in_=xT[:, :, 0:NT]), reads=['xT'], writes=['yT'])
        S.barrier()
        nc.sync.wait_ge(ccsem, ccn[0])
    return nc


def t5_bucket_np(n):
    n = np.maximum(n, 0)
    max_exact = 16
    nf = np.maximum(n, 1).astype(np.float32)
    large = max_exact + (np.log(nf / np.float32(max_exact)) / np.float32(math.log(2048 / max_exact)) * np.float32(32 - max_exact)).astype(np.int32)
    large = np.minimum(large, 31)
    return np.where(n < max_exact, n, large)


_CACHE = {}


def kernel(x_prompt, x_sample, cache_kv_w128, cache_kv_w512, cache_kv_w2048, state_rglru_h,
           state_rglru_conv, attn_norm, attn_w_in, attn_q_norm, attn_k_norm, attn_w_out, rel_bias,
           rnn_norm, rnn_w_in, rnn_conv_w, rnn_conv_b, rnn_gate_a_w, rnn_gate_a_b, rnn_gate_x_w,
           rnn_gate_x_b, rnn_lambda, rnn_w_out):
    f = lambda a: np.ascontiguousarray(np.asarray(a, dtype=np.float32))
    x_prompt, x_sample = f(x_prompt), f(x_sample)
    caches = [f(cache_kv_w128), f(cache_kv_w512), f(cache_kv_w2048)]
    if 'nc' not in _CACHE:
        _CACHE['nc'] = build_program()
    nc = _CACHE['nc']
    ohr = np.zeros((33, 3, 384), np.float32)
    for g in range(3):
        for e in range(384):
            step = e - 127
            row = int(t5_bucket_np(np.array([step * DIL[g]]))[0]) if 0 <= step <= 128 else 32
            ohr[row, g, 383 - e] = 1.0
    negrow = np.full((1, 24), NEG, np.float32)

    def chan(a):
        a = f(a)
        return np.ascontiguousarray(np.swapaxes(a.reshape(a.shape[:-1] + (10, 128)), -1, -2))

    shared = {
        "a_norm": np.ascontiguousarray(np.swapaxes(f(attn_norm).reshape(2, 8, 128), 1, 2)),
        "a_win": f(attn_w_in),
        "a_qn": np.ascontiguousarray(np.broadcast_to(f(attn_q_norm)[:, None, :], (2, 128, 128))),
        "a_kn": np.ascontiguousarray(np.broadcast_to(f(attn_k_norm)[:, None, :], (2, 128, 128))),
        "a_wout": f(attn_w_out),
        "relb": f(rel_bias),
        "r_norm": np.ascontiguousarray(np.swapaxes(f(rnn_norm).reshape(2, 8, 128), 1, 2)),
        "r_win": f(rnn_w_in),
        "r_cw": np.ascontiguousarray(np.transpose(f(rnn_conv_w).reshape(2, 4, 10, 128), (0, 3, 2, 1))),
        "r_cb": chan(rnn_conv_b),
        "r_gaw": f(rnn_gate_a_w),
        "r_gab": np.ascontiguousarray(np.swapaxes(f(rnn_gate_a_b), 1, 2)),
        "r_gxw": f(rnn_gate_x_w),
        "r_gxb": np.ascontiguousarray(np.swapaxes(f(rnn_gate_x_b), 1, 2)),
        "r_lam": chan(rnn_lambda),
        "r_wout": f(rnn_w_out),
        "ohr": ohr,
        "negrow": negrow,
    }
    sh = f(state_rglru_h)
    sc = f(state_rglru_conv)
    in_maps = []
    for c in range(NCORE):
        b, q = divmod(c, 4)
        xs = np.concatenate([x_prompt[b, q * NP:(q + 1) * NP], x_sample[c * 4:(c + 1) * 4].reshape(NS, 1024)], axis=0)
        xT = np.ascontiguousarray(xs.T.reshape(8, 128, NT).transpose(1, 0, 2))
        m = dict(shared)
        m["xT_in"] = xT
        for g in range(3):
            m["cache%d" % g] = np.ascontiguousarray(caches[g][:, c * 4:(c + 1) * 4])
        hs_ = sh[:, c * 4:(c + 1) * 4].reshape(2, 4, 10, 128)
        m["h0s"] = np.ascontiguousarray(hs_.transpose(0, 3, 2, 1))
        cs_ = sc[:, c * 4:(c + 1) * 4].reshape(2, 4, 3, 10, 128)
        m["cst"] = np.ascontiguousarray(cs_.transpose(0, 4, 3, 1, 2))
        mfv = np.zeros((128, 12), np.float32)
        first = (q == 0)
        mfv[:, 0] = NEG if first else 0.0
        mfv[:, 1] = 0.0 if first else 1.0
        mfv[:, 11] = EPS
        for r in range(NCORE):
            if r // 4 == b and r < c:
                mfv[:, 2 + r] = 1.0
        m["meta_f"] = mfv
        prev = c - 1 if not first else c
        miv = np.zeros((128, 2), np.int32)
        miv[:, 0] = prev * (NHB * 16 * 128) + np.arange(128)
        miv[:, 1] = prev * 128 + np.arange(128)
        m["meta_i"] = miv
        in_maps.append(m)
    res = run_bass_kernel_spmd(nc, in_maps, core_ids=list(range(NCORE))).results
    y_prompt = np.zeros((2, 8192, 1024), np.float32)
    y_sample = np.zeros((32, 8, 1024), np.float32)
    for c in range(NCORE):
        b, q = divmod(c, 4)
        yT = res[c]["yT_out"]
        y = yT.transpose(2, 1, 0).reshape(NT, 1024)
        y_prompt[b, q * NP:(q + 1) * NP] = y[:NP]
        y_sample[c * 4:(c + 1) * 4] = y[NP:].reshape(4, 8, 1024)
    outs = [y_prompt, y_sample]
    for g in range(3):
        outs.append(np.ascontiguousarray(np.stack([res[3]["kvp%d" % g], res[7]["kvp%d" % g]], axis=1)))
        outs.append(np.ascontiguousarray(np.concatenate([res[c]["kvs%d" % g] for c in range(NCORE)], axis=1)))

    def unchan(a):
        return np.ascontiguousarray(np.swapaxes(a, -1, -2).reshape(a.shape[:-2] + (1280,)))
    h_p = np.stack([unchan(res[3]["hp_out"]), unchan(res[7]["hp_out"])], axis=1)
    h_s = np.concatenate([unchan(np.transpose(res[c]["hs_out"], (0, 3, 1, 2))) for c in range(NCORE)], axis=1)
    cpf = lambda a: np.ascontiguousarray(np.transpose(a, (0, 3, 1, 2)).reshape(2, 3, 1280))
    c_p = np.stack([cpf(res[3]["cp_out"]), cpf(res[7]["cp_out"])], axis=1)
    csf = lambda a: np.ascontiguousarray(np.transpose(a, (0, 3, 4, 1, 2)).reshape(2, 4, 3, 1280))
    c_s = np.concatenate([csf(res[c]["cs_out"]) for c in range(NCORE)], axis=1)
    outs += [h_p.astype(np.float32), h_s.astype(np.float32), c_p.astype(np.float32), c_s.astype(np.float32)]
    return tuple(outs)
```

```python
import contextlib
import math
import numpy as np
import concourse.bass as bass
import concourse.mybir as mybir
from concourse.bass_utils import run_bass_kernel_spmd

F32, BF16, I32 = mybir.dt.float32, mybir.dt.bfloat16, mybir.dt.int32
AF = mybir.ActivationFunctionType
ALU = mybir.AluOpType
AX = mybir.AxisListType

NCORE = 8
NP, NS, NSQ = 2048, 32, 4
NT = NP + NS
NX = NT + 3
DIL = (1, 4, 16)
WB = (128, 512, 2048)
NRES = (1, 4, 8)
NQ = (8, 2, 1)
HPB = (1, 4, 16)
HB0 = (0, 1, 5)
NHB = 21
LX = 3 + NP + NSQ * 11
EPS = 1e-6
NEG = -1e30
SCALE = 128 ** -0.5
NDS = 40
TILES = [(0, 512), (512, 512), (1024, 512), (1536, 512), (2048, NX - 2048)]
XTILES = [(0, 512), (512, 512), (1024, 512), (1536, 512), (2048, LX - 2048)]


class Sched:
    def __init__(self, nc, es):
        self.nc = nc
        self.eng = {'pe': nc.tensor, 'act': nc.scalar, 'dve': nc.vector, 'pool': nc.gpsimd, 'sp': nc.sync}
        self.sem = {k: es.enter_context(nc.semaphore("s_" + k)) for k in self.eng}
        self.cnt = {k: 0 for k in self.eng}
        self.dsem = [es.enter_context(nc.semaphore("d%d" % i)) for i in range(NDS)]
        self.dcnt = [0] * NDS
        self.dnext = 0
        self.waited = {k: {} for k in self.eng}
        self.state = {}

    def _wait(self, e, tok):
        sem, val, sid = tok
        if self.waited[e].get(sid, 0) >= val:
            return
        self.eng[e].wait_ge(sem, val)
        self.waited[e][sid] = val

    def _deps(self, e, reads, writes):
        for k in reads:
            st = self.state.get(k)
            if st and st['w']:
                self._wait(e, st['w'])
        for k in writes:
            st = self.state.get(k)
            if st:
                if st['w']:
                    self._wait(e, st['w'])
                for t in st['r'].values():
                    self._wait(e, t)

    def _commit(self, tok, reads, writes):
        for k in reads:
            st = self.state.setdefault(k, {'w': None, 'r': {}})
            st['r'][tok[2]] = tok
        for k in writes:
            self.state[k] = {'w': tok, 'r': {}}

    def op(self, e, fn, reads=(), writes=()):
        self._deps(e, reads, writes)
        ins = fn(self.eng[e])
        self.cnt[e] += 1
        ins.then_inc(self.sem[e], 1)
        tok = (self.sem[e], self.cnt[e], e)
        self._commit(tok, reads, writes)

    def dma(self, e, fn, reads=(), writes=(), inc=16):
        self._deps(e, reads, writes)
        i = self.dnext
        self.dnext = (i + 1) % NDS
        if self.dcnt[i] > 0:
            self._wait(e, (self.dsem[i], self.dcnt[i], 'd%d' % i))
        self.dcnt[i] += inc
        ins = fn(self.eng[e])
        ins.then_inc(self.dsem[i], inc)
        tok = (self.dsem[i], self.dcnt[i], 'd%d' % i)
        self._commit(tok, reads, writes)

    def barrier(self):
        toks = [(self.sem[k], self.cnt[k], k) for k in self.eng if self.cnt[k] > 0]
        toks += [(self.dsem[i], self.dcnt[i], 'd%d' % i) for i in range(NDS) if self.dcnt[i] > 0]
        for e in self.eng:
            for t in toks:
                self._wait(e, t)
        self.state = {}


class Arena:
    def __init__(self, ap, nwords):
        self.ap, self.n, self.top = ap, nwords, 0

    def _alloc(self, nbytes):
        w = (nbytes + 31) // 32 * 8
        off = self.top
        self.top += w
        assert self.top <= self.n, ("arena overflow", self.top, self.n)
        return off, w

    def f32(self, n):
        off, w = self._alloc(4 * n)
        return self.ap[:, off:off + n]

    def i32(self, n):
        off, w = self._alloc(4 * n)
        return self.ap[:, off:off + n].bitcast(I32)

    def bf16(self, n):
        off, w = self._alloc(2 * n)
        return self.ap[:, off:off + w].bitcast(BF16)[:, 0:n]


def build_program():
    nc = bass.Bass("TRN2", target_bir_lowering=False)

    def din(name, shape, dt=F32):
        return nc.dram_tensor(name, list(shape), dt, kind="ExternalInput").ap()

    def dout(name, shape, dt=F32):
        return nc.dram_tensor(name, list(shape), dt, kind="ExternalOutput").ap()

    def dscr(name, shape, dt=BF16):
        return nc.dram_tensor(name, list(shape), dt, kind="Internal").ap()

    xT_in = din("xT_in", [128, 8, NT])
    cache = [din("cache%d" % g, [2, NSQ, WB[g], 2, 8, 128]) for g in range(3)]
    h0s_in = din("h0s", [2, 128, 10, 4])
    cst_in = din("cst", [2, 128, 10, 4, 3])
    a_norm = din("a_norm", [2, 128, 8])
    a_win = din("a_win", [2, 1024, 10240])
    a_qn = din("a_qn", [2, 128, 128])
    a_kn = din("a_kn", [2, 128, 128])
    a_wout = din("a_wout", [2, 1024, 1024])
    relb = din("relb", [32, 24])
    r_norm = din("r_norm", [2, 128, 8])
    r_win = din("r_win", [2, 1024, 2560])
    r_cw = din("r_cw", [2, 128, 10, 4])
    r_cb = din("r_cb", [2, 128, 10])
    r_gaw = din("r_gaw", [2, 10, 128, 128])
    r_gab = din("r_gab", [2, 128, 10])
    r_gxw = din("r_gxw", [2, 10, 128, 128])
    r_gxb = din("r_gxb", [2, 128, 10])
    r_lam = din("r_lam", [2, 128, 10])
    r_wout = din("r_wout", [2, 1280, 1024])
    ohr_in = din("ohr", [33, 3, 384])
    negrow = din("negrow", [1, 24])
    meta_f = din("meta_f", [128, 12])
    meta_i = din("meta_i", [128, 2], I32)
    yT_out = dout("yT_out", [128, 8, NT])
    kvp = [dout("kvp%d" % g, [2, WB[g], 2, 8, 128]) for g in range(3)]
    kvs = [dout("kvs%d" % g, [2, NSQ, WB[g], 2, 8, 128]) for g in range(3)]
    hp_out = dout("hp_out", [2, 128, 10])
    hs_out = dout("hs_out", [2, 128, 10, 4])
    cp_out = dout("cp_out", [2, 10, 128, 3])
    cs_out = dout("cs_out", [2, 10, 128, 4, 3])
    QTs = dscr("QTs", [3, 8, 128, NT])
    KTs = dscr("KTs", [3, 8, 128, NT])
    Vs = dscr("Vs", [3, NT, 1024])
    SKT = [dscr("SKT%d" % g, [8, NSQ * NRES[g], 128, 128]) for g in range(3)]
    SV = [dscr("SV%d" % g, [8, NSQ * NRES[g], 128, 128]) for g in range(3)]
    BTs = dscr("BTs", [3, 8, 128, 256], F32)
    hsend = [dscr("hsend%d" % l, [NHB * 16 * 128, 128]) for l in range(2)]
    hall = [nc.dram_tensor("hall%d" % l, [NCORE * NHB * 16 * 128, 128], BF16, kind="Internal", addr_space="Local").ap() for l in range(2)]
    xsend = [dscr("xsend%d" % l, [128, 24], F32) for l in range(2)]
    xall = [nc.dram_tensor("xall%d" % l, [NCORE * 128, 24], F32, kind="Internal", addr_space="Local").ap() for l in range(2)]
    asend = [dscr("asend%d" % l, [128, 20], F32) for l in range(2)]
    aall = [nc.dram_tensor("aall%d" % l, [NCORE * 128, 20], F32, kind="Internal", addr_space="Local").ap() for l in range(2)]
    RG = dscr("RG", [10, 128, LX])
    RC = dscr("RC", [10, 128, LX])

    es = contextlib.ExitStack()
    with es:
        es.enter_context(nc.allow_non_contiguous_dma(reason="small strided state/aggregate transfers"))
        AW = 47616
        arena_t = es.enter_context(nc.sbuf_tensor("arena", [128, AW], F32))
        ar = Arena(arena_t[:, :], AW)
        ps = [es.enter_context(nc.psum_tensor("ps%d" % i, [128, 512], F32)) for i in range(8)]
        S = Sched(nc, es)
        ccsem = es.enter_context(nc.semaphore("ccsem"))
        ccn = [0]
        ALLG = [list(range(NCORE))]

        xT = ar.f32(8 * NX).rearrange("p (c t) -> p c t", c=8)
        ident = ar.bf16(128)
        ones = ar.bf16(128)
        mf = ar.f32(12)
        mi = ar.i32(2)
        persist_top = ar.top
        negf, notfirst, zero1, epsc = mf[:, 0:1], mf[:, 1:2], mf[:, 10:11], mf[:, 11:12]

        S.dma('sp', lambda e: e.dma_start(out=xT[:, :, 0:NT], in_=xT_in), writes=['xT'])
        S.dma('sp', lambda e: e.dma_start(out=mf, in_=meta_f), writes=['mf'])
        S.dma('sp', lambda e: e.dma_start(out=mi, in_=meta_i), writes=['mi'])
        S.op('pool', lambda e: e.memset(xT[:, :, NT:NX], 0.0), writes=['xTh'])
        S.op('pool', lambda e: e.memset(ones, 1.0), writes=['ones'])
        idf = ar.f32(128)
        S.op('pool', lambda e: e.iota(idf, pattern=[[1, 128]], base=0, channel_multiplier=-1, allow_small_or_imprecise_dtypes=True), writes=['idf'])
        S.op('dve', lambda e: e.tensor_scalar(out=ident, in0=idf, scalar1=0.0, scalar2=0.0, op0=ALU.is_equal, op1=ALU.add), reads=['idf'], writes=['ident'])
        m0 = ar.top
        tabx = ar.f32(24)
        ohr = ar.f32(3 * 384).rearrange("p (g i) -> p g i", g=3)
        bttmp = ar.f32(8 * 256).rearrange("p (h w a) -> p h w a", h=8, w=2)
        S.dma('sp', lambda e: e.dma_start(out=tabx[0:32, :], in_=relb), writes=['tabx'])
        S.dma('sp', lambda e: e.dma_start(out=ohr[0:33], in_=ohr_in), writes=['ohr'])
        S.op('act', lambda e: e.activation(out=tabx[0:32, :], in_=tabx[0:32, :], func=AF.Copy, scale=1.0 / SCALE), reads=['tabx'], writes=['tabx'])
        S.dma('sp', lambda e: e.dma_start(out=tabx[32:33, :], in_=negrow), writes=['tabx2'])
        for g in range(3):
            for w in range(2):
                for half in range(2):
                    pt = ps[half]

                    def mm(e, g=g, w=w, half=half, pt=pt):
                        last = None
                        for al in range(64):
                            a = half * 64 + al
                            i0 = (256 - a) if w == 1 else (128 - a)
                            last = e.matmul(pt[:, al * 8:al * 8 + 8], lhsT=ohr[0:33, g, i0:i0 + 128], rhs=tabx[0:33, g * 8:g * 8 + 8], start=True, stop=True)
                        return last
                    S.op('pe', mm, reads=['ohr', 'tabx', 'tabx2'], writes=[('ps', half)])
                    S.op('act', lambda e, w=w, half=half, pt=pt: e.activation(
                        out=bttmp[:, :, w, half * 64:half * 64 + 64], in_=pt[:, :].rearrange("p (a h) -> p h a", h=8), func=AF.Copy),
                        reads=[('ps', half)], writes=['bttmp'])
            S.dma('pool', lambda e, g=g: e.dma_start(out=BTs[g].rearrange("h k x -> k h x"), in_=bttmp.rearrange("p h w a -> p h (w a)")), reads=['bttmp'], writes=['BTs'])
        S.barrier()
        ar.top = m0

        wrot = [0]

        def load_w(dst, src, kc, ncols, key, ceng='pool'):
            cpp = max(1, 1024 // ncols)
            c = 0
            while c < kc:
                n = min(cpp, kc - c)
                sidx = wrot[0] % 2
                wrot[0] += 1
                st = wstage[sidx][:, 0:n * ncols].rearrange("p (c n) -> p c n", c=n)
                S.dma('sp', lambda e, st=st, c=c, n=n: e.dma_start(out=st, in_=src[c * 128:(c + n) * 128, :].rearrange("(c p) n -> p c n", p=128)), writes=[('wst', sidx)])
                if ceng == 'act':
                    S.op('act', lambda e, st=st, c=c, n=n: e.activation(out=dst[:, c:c + n, :], in_=st, func=AF.Copy), reads=[('wst', sidx)], writes=[key])
                else:
                    S.op(ceng, lambda e, st=st, c=c, n=n: e.tensor_copy(out=dst[:, c:c + n, :], in_=st), reads=[('wst', sidx)], writes=[key])
                c += n

        def rmsnorm(gam):
            for ti, (t0, n) in enumerate(TILES):
                pt = ps[ti % 2]
                S.op('act', lambda e, t0=t0, n=n: e.activation(out=sqb[:, :, 0:n], in_=xT[:, :, t0:t0 + n], func=AF.Square), reads=['xT', 'xTh'], writes=['sqb'])

                def mm(e, n=n, pt=pt):
                    last = None
                    for c in range(8):
                        last = e.matmul(pt[:, 0:n], lhsT=ones.to_broadcast([128, 128]) if False else onesq[:, :], rhs=sqb[:, c, 0:n], start=(c == 0), stop=(c == 7))
                    return last
                S.op('pe', mm, reads=['sqb', 'onesq'], writes=[('ps', ti % 2)])
                S.op('act', lambda e, n=n, pt=pt: e.activation(out=rstd[:, 0:n], in_=pt[:, 0:n], func=AF.Sqrt, bias=epsc, scale=1.0 / 1024), reads=[('ps', ti % 2), 'mf'], writes=['rstd'])
                S.op('dve', lambda e, n=n: e.reciprocal(out=rstd[:, 0:n], in_=rstd[:, 0:n]), reads=['rstd'], writes=['rstd'])
                for c in range(8):
                    S.op('dve', lambda e, c=c, t0=t0, n=n: e.scalar_tensor_tensor(out=xn[:, c, t0:t0 + n], in0=xT[:, c, t0:t0 + n], scalar=gam[:, c:c + 1], in1=rstd[:, 0:n], op0=ALU.mult, op1=ALU.mult), reads=['xT', 'xTh', 'rstd', 'gam'], writes=['xn'])

        ar.top = persist_top
        onesq = ar.bf16(128)
        identq = ident
        S.op('pool', lambda e: e.memset(onesq, 1.0), writes=['onesq'])
        wstage = [ar.f32(1024), ar.f32(1024)]
        persist_top = ar.top

        def attn_layer(li):
            ar.top = persist_top
            nonlocal_names = {}
            global_refs = {}
            G = ar.bf16(8 * NT).rearrange("p (h t) -> p h t", h=8)
            PB0 = ar.top
            xn_ = ar.bf16(8 * NX).rearrange("p (c t) -> p c t", c=8)
            PB1 = ar.top
            sqb_ = ar.bf16(8 * 512).rearrange("p (c t) -> p c t", c=8)
            rstd_ = ar.f32(512)
            gam = ar.f32(8)
            gq = ar.f32(128)
            gk = ar.f32(128)
            Wb = [ar.bf16(8 * 512).rearrange("p (c n) -> p c n", c=8) for _ in range(2)]
            kf = ar.f32(512)
            sq = ar.f32(512)
            ssq = ar.f32(4)
            knb = ar.bf16(512)
            ktb = ar.bf16(512)
            P1 = ar.top
            return G, PB0, xn_, PB1, sqb_, rstd_, gam, gq, gk, Wb, kf, sq, ssq, knb, ktb, P1

        for layer in range(4):
            li = layer // 2
            if layer % 2 == 0:
                (G, PB0, xn, PB1, sqb, rstd, gam, gq, gk, Wb, kf, sq, ssq, knb, ktb, P1) = attn_layer(li)
                S.dma('sp', lambda e: e.dma_start(out=gam, in_=a_norm[li]), writes=['gam'])
                S.dma('sp', lambda e: e.dma_start(out=gq, in_=a_qn[li]), writes=['gq'])
                S.dma('sp', lambda e: e.dma_start(out=gk, in_=a_kn[li]), writes=['gk'])
                rmsnorm(gam)
                for half in range(2):
                    W = Wb[half]
                    load_w(W, a_win[li][:, 9216 + half * 512:9216 + half * 512 + 512], 8, 512, ('W', half))
                    for hh in range(4):
                        h = half * 4 + hh
                        for ti, (t0, n) in enumerate(TILES):
                            n = min(n, NT - t0)
                            pi = 2 + (ti % 2)

                            def mm(e, W=W, hh=hh, t0=t0, n=n, pi=pi):
                                last = None
                                for c in range(8):
                                    last = e.matmul(ps[pi][:, 0:n], lhsT=W[:, c, hh * 128:hh * 128 + 128], rhs=xn[:, c, t0:t0 + n], start=(c == 0), stop=(c == 7))
                                return last
                            S.op('pe', mm, reads=[('W', half), 'xn'], writes=[('ps', pi)])
                            S.op('act', lambda e, h=h, t0=t0, n=n, pi=pi: e.activation(out=G[:, h, t0:t0 + n], in_=ps[pi][:, 0:n], func=AF.Silu), reads=[('ps', pi)], writes=['G'])
                kf2 = [kf, ar.f32(512)]
                sq2 = [sq, ar.f32(512)]
                ssq2 = [ssq, ar.f32(4)]
                knb2 = [knb, ar.bf16(512)]
                ktb2 = [ktb, ar.bf16(512)]
                units1 = [(g, typ, half) for g in range(3) for typ in range(2) for half in range(2)] + [(g, 2, half) for g in range(3) for half in range(2)]
                NKV = 12
                hskeys = []
                ccpieces = []
                if li == 0:
                    for g_ in range(3):
                        for l2 in range(2):
                            for s_ in range(NSQ):
                                n = WB[g_] - 8
                                r0 = 0
                                while r0 < n:
                                    nr = min(256, n - r0)
                                    ccpieces.append((g_, l2, s_, r0, nr))
                                    r0 += nr

                def hk(row):
                    hskeys.append(('hs', row))
                    return ('hs', row)

                def w_src(g, typ, half):
                    colbase = (3072, 6144, 0)[typ] + g * 1024
                    return a_win[li][:, colbase + half * 512:colbase + half * 512 + 512]
                load_w(Wb[0], w_src(*units1[0]), 8, 512, ('W', 0))
                it = 0
                for ui1, (g, typ, half) in enumerate(units1):
                    d = DIL[g]
                    nb = 16 // d
                    blocks = []
                    for bi in range(16):
                        r, n_ = divmod(bi, nb)
                        c0 = r + n_ * 128 * d
                        blocks.append((bi, slice(c0, c0 + 127 * d + 1, d), 128))
                    blocks.append((16, slice(NP, NT), 32))
                    W = Wb[ui1 % 2]
                    wkey = ('W', ui1 % 2)
                    if ui1 == NKV:
                        S.op('pool', lambda e: e.collective_compute("AllGather", ALU.bypass, replica_groups=ALLG, ins=[hsend[li]], outs=[hall[li]]), reads=hskeys, writes=['hall'])
                    if ui1 + 1 < len(units1):
                        load_w(Wb[(ui1 + 1) % 2], w_src(*units1[ui1 + 1]), 8, 512, ('W', (ui1 + 1) % 2), ceng=('act' if ui1 + 1 >= NKV else 'pool'))
                    for (bi, cs, M) in blocks:
                        b2 = it % 2
                        it += 1
                        if ccpieces and it % 3 == 0:
                            g_, l2, s_, r0, nr = ccpieces.pop(0)
                            nc.scalar.dma_start(out=kvs[g_][l2, s_, r0:r0 + nr].rearrange("r a h d -> r (a h d)"),
                                                in_=cache[g_][l2, s_, 8 + r0:8 + r0 + nr].rearrange("r a h d -> r (a h d)")).then_inc(ccsem, 16)
                            ccn[0] += 16
                        pi = b2
                        pt = ps[pi]
                        kf_, sq_, ssq_, knb_, ktb_ = kf2[b2], sq2[b2], ssq2[b2], knb2[b2], ktb2[b2]
                        kfk, sqk, ssqk, knbk, ktbk = ('kf', b2), ('sq', b2), ('ssq', b2), ('knb', b2), ('ktb', b2)

                        def mm(e, W=W, cs=cs, M=M, pt=pt):
                            last = None
                            for c in range(8):
                                last = e.matmul(pt[0:M, :], lhsT=xn[:, c, cs], rhs=W[:, c, :], start=(c == 0), stop=(c == 7))
                            return last
                        S.op('pe', mm, reads=[wkey, 'xn'], writes=[('ps', pi)])
                        tcol = bi * 128
                        is_halo = (bi < 16) and ((bi % nb) == nb - 1)
                        need_f32 = is_halo or bi == 16
                        hb = HB0[g] + (bi // nb)
                        if typ == 1:
                            if need_f32:
                                S.op('act', lambda e, M=M, pt=pt, kf_=kf_: e.activation(out=kf_[0:M, :], in_=pt[0:M, :], func=AF.Copy), reads=[('ps', pi)], writes=[kfk])
                                if is_halo:
                                    r = bi // nb
                                    S.dma('sp', lambda e, r=r, d=d, kf_=kf_, g=g, half=half: e.dma_start(
                                        out=kvp[g][li, r:r + 127 * d + 1:d, 1, half * 4:half * 4 + 4, :], in_=kf_[:, :].rearrange("p (h d) -> p h d", h=4)), reads=[kfk], writes=[('kvp', g, bi, typ, half)])
                                else:
                                    for s_ in range(NSQ):
                                        S.dma('sp', lambda e, s_=s_, kf_=kf_, g=g, half=half: e.dma_start(
                                            out=kvs[g][li, s_, WB[g] - 8:WB[g], 1, half * 4:half * 4 + 4, :], in_=kf_[s_ * 8:s_ * 8 + 8, :].rearrange("p (h d) -> p h d", h=4)), reads=[kfk], writes=[('kvs', g, s_, typ, half)])
                            S.op('act', lambda e, M=M, pt=pt, knb_=knb_: e.activation(out=knb_[0:M, :], in_=pt[0:M, :], func=AF.Copy), reads=[('ps', pi)], writes=[knbk])
                            S.dma('act', lambda e, M=M, tcol=tcol, knb_=knb_, g=g, half=half: e.dma_start(out=Vs[g, tcol:tcol + M, half * 512:half * 512 + 512], in_=knb_[0:M, :]), reads=[knbk], writes=[('Vs', g, bi, half)])
                            if is_halo:
                                for hh in range(4):
                                    row = ((hb * 8 + half * 4 + hh) * 2 + 1) * 128
                                    S.dma('act', lambda e, hh=hh, row=row, knb_=knb_: e.dma_start(out=hsend[li][row:row + 128, :], in_=knb_[:, hh * 128:hh * 128 + 128]), reads=[knbk], writes=[hk(row)])
                            continue
                        gvec = gk if typ == 0 else gq
                        v4 = lambda t, M=M: t[0:M, :].rearrange("p (h d) -> p h d", h=4)
                        S.op('act', lambda e, M=M, pt=pt, kf_=kf_: e.activation(out=kf_[0:M, :], in_=pt[0:M, :], func=AF.Copy), reads=[('ps', pi)], writes=[kfk])
                        S.op('dve', lambda e, M=M, kf_=kf_, sq_=sq_: e.tensor_tensor(out=sq_[0:M, :], in0=kf_[0:M, :], in1=kf_[0:M, :], op=ALU.mult), reads=[kfk], writes=[sqk])
                        S.op('dve', lambda e, M=M, sq_=sq_, ssq_=ssq_: e.tensor_reduce(out=ssq_[0:M, :], in_=sq_[0:M, :].rearrange("p (h d) -> p h d", h=4), axis=AX.X, op=ALU.add), reads=[sqk], writes=[ssqk])
                        S.op('act', lambda e, M=M, ssq_=ssq_: e.activation(out=ssq_[0:M, :], in_=ssq_[0:M, :], func=AF.Sqrt, bias=epsc[0:M, :], scale=1.0 / 128), reads=[ssqk, 'mf'], writes=[ssqk])
                        S.op('dve', lambda e, M=M, ssq_=ssq_: e.reciprocal(out=ssq_[0:M, :], in_=ssq_[0:M, :]), reads=[ssqk], writes=[ssqk])
                        S.op('dve', lambda e, M=M, kf_=kf_, ssq_=ssq_, v4=v4: e.tensor_tensor(out=v4(kf_), in0=v4(kf_), in1=ssq_[0:M, :].unsqueeze(2).to_broadcast([M, 4, 128]), op=ALU.mult), reads=[kfk, ssqk], writes=[kfk])
                        if typ == 0 and need_f32:
                            S.op('dve', lambda e, M=M, kf_=kf_, gvec=gvec, v4=v4: e.tensor_tensor(out=v4(kf_), in0=v4(kf_), in1=gvec[0:M, :].unsqueeze(1).to_broadcast([M, 4, 128]), op=ALU.mult), reads=[kfk, 'gq', 'gk'], writes=[kfk])
                            if is_halo:
                                r = bi // nb
                                S.dma('sp', lambda e, r=r, d=d, kf_=kf_, g=g, half=half: e.dma_start(
                                    out=kvp[g][li, r:r + 127 * d + 1:d, 0, half * 4:half * 4 + 4, :], in_=kf_[:, :].rearrange("p (h d) -> p h d", h=4)), reads=[kfk], writes=[('kvp', g, bi, typ, half)])
                            else:
                                for s_ in range(NSQ):
                                    S.dma('sp', lambda e, s_=s_, kf_=kf_, g=g, half=half: e.dma_start(
                                        out=kvs[g][li, s_, WB[g] - 8:WB[g], 0, half * 4:half * 4 + 4, :], in_=kf_[s_ * 8:s_ * 8 + 8, :].rearrange("p (h d) -> p h d", h=4)), reads=[kfk], writes=[('kvs', g, s_, typ, half)])
                            S.op('act', lambda e, M=M, kf_=kf_, knb_=knb_: e.activation(out=knb_[0:M, :], in_=kf_[0:M, :], func=AF.Copy), reads=[kfk], writes=[knbk])
                        else:
                            S.op('dve', lambda e, M=M, kf_=kf_, knb_=knb_, gvec=gvec, v4=v4: e.tensor_tensor(out=v4(knb_), in0=v4(kf_), in1=gvec[0:M, :].unsqueeze(1).to_broadcast([M, 4, 128]), op=ALU.mult), reads=[kfk, 'gq', 'gk'], writes=[knbk])
                        ptb = ps[4 + pi][:, :].bitcast(BF16)

                        def tr(e, M=M, ptb=ptb, knb_=knb_):
                            last = None
                            for hh in range(4):
                                last = e.transpose(ptb[:, hh * 128:hh * 128 + M], knb_[0:M, hh * 128:hh * 128 + 128], ident[0:M, 0:M])
                            return last
                        S.op('pe', tr, reads=[knbk, 'ident'], writes=[('ps', 4 + pi)])
                        S.op('act', lambda e, M=M, ptb=ptb, ktb_=ktb_: e.activation(out=ktb_[:, :].rearrange("p (h t) -> p h t", h=4)[:, :, 0:M], in_=ptb[:, 0:512].rearrange("p (h t) -> p h t", h=4)[:, :, 0:M], func=AF.Copy),
                             reads=[('ps', 4 + pi)], writes=[ktbk])
                        dst = KTs if typ == 0 else QTs
                        S.dma('act', lambda e, M=M, tcol=tcol, dst=dst, ktb_=ktb_, g=g, half=half: e.dma_start(
                            out=dst[g, half * 4:half * 4 + 4, :, tcol:tcol + M].rearrange("h d t -> d h t"), in_=ktb_[:, :].rearrange("p (h t) -> p h t", h=4)[:, :, 0:M]),
                            reads=[ktbk], writes=[('QK', typ, g, bi, half)])
                        if typ == 0 and is_halo:
                            for hh in range(4):
                                row = ((hb * 8 + half * 4 + hh) * 2 + 0) * 128
                                S.dma('act', lambda e, hh=hh, row=row, ktb_=ktb_: e.dma_start(out=hsend[li][row:row + 128, :], in_=ktb_[:, hh * 128:hh * 128 + 128]), reads=[ktbk], writes=[hk(row)])
                while ccpieces:
                    g_, l2, s_, r0, nr = ccpieces.pop(0)
                    nc.scalar.dma_start(out=kvs[g_][l2, s_, r0:r0 + nr].rearrange("r a h d -> r (a h d)"),
                                        in_=cache[g_][l2, s_, 8 + r0:8 + r0 + nr].rearrange("r a h d -> r (a h d)")).then_inc(ccsem, 16)
                    ccn[0] += 16
                S.barrier()
                ar.top = PB0
                acc = ar.f32(2 * NT).rearrange("p (w t) -> p w t", w=2)
                accS = ar.f32(8 * 2 * 32).rearrange("p (h w t) -> p h w t", h=8, w=2)
                NB = 4
                stt = [ar.f32(256).rearrange("p (w a) -> p w a", w=2) for _ in range(NB)]
                pT = [ar.bf16(256).rearrange("p (w a) -> p w a", w=2) for _ in range(NB)]
                P2 = ar.top
                ck = [ar.f32(2048) for _ in range(2)]
                ckb = [ar.bf16(2048) for _ in range(2)]
                ckt = [ar.bf16(1024).rearrange("p (h k) -> p h k", h=8) for _ in range(2)]
                vcs = [ar.bf16(1024) for _ in range(2)]
                QS = ar.bf16(3 * 8 * 32).rearrange("p (g h t) -> p g h t", g=3, h=8)
                KS = ar.bf16(3 * 8 * 32).rearrange("p (g h t) -> p g h t", g=3, h=8)
                BTa = ar.f32(3 * 8 * 16).rearrange("p (g h w a) -> p g h w a", g=3, h=8, w=2)
                LAG = 3
                for g in range(3):
                    for hh8 in range(0, 8, 4):
                        S.dma('sp', lambda e, g=g, hh8=hh8: e.dma_start(out=QS[:, g, hh8:hh8 + 4, :], in_=QTs[g, hh8:hh8 + 4, :, NP:NT].rearrange("h d t -> d h t")), writes=['QS'])
                        S.dma('sp', lambda e, g=g, hh8=hh8: e.dma_start(out=KS[:, g, hh8:hh8 + 4, :], in_=KTs[g, hh8:hh8 + 4, :, NP:NT].rearrange("h d t -> d h t")), writes=['KS'])
                    for w_ in range(2):
                        S.dma('sp', lambda e, g=g, w_=w_: e.dma_start(out=BTa[:, g, :, w_, :], in_=BTs[g][:, :, w_ * 128:w_ * 128 + 8].rearrange("h k a -> k h a")), writes=['BTa'])
                S.op('dve', lambda e: e.memset(accS, 0.0), writes=['accS'])
                pipe = []
                cnt2 = [0]

                def stageA(job):
                    i = job['i']
                    b4 = i % NB
                    pss = ps[b4]
                    st_, pT_ = stt[b4], pT[b4]
                    M, nkc = job['M'], job['nkc']
                    S.op('pe', lambda e: (e.matmul(pss[:, 0:M], lhsT=job['kprev'], rhs=job['q'], start=True, stop=True),
                                          e.matmul(pss[0:nkc, 128:128 + M], lhsT=job['kcur'], rhs=job['q'], start=True, stop=True))[1],
                         reads=job['rd1'], writes=[('ps', b4)])
                    bt = job['bt']
                    if job['merge']:
                        S.op('dve', lambda e: e.tensor_tensor(out=st_[:, :, :], in0=pss[:, 0:256].rearrange("p (w a) -> p w a", w=2), in1=bt[:, :, :], op=ALU.add), reads=[('ps', b4)] + job['rdbt'], writes=[('st', b4), ('st2', b4)])
                        S.op('act', lambda e: e.activation(out=pT_[:, :, :], in_=st_[:, :, :], func=AF.Exp, scale=SCALE), reads=[('st', b4), ('st2', b4)], writes=[('pT', b4), ('pT2', b4)])
                    else:
                        S.op('dve', lambda e: e.tensor_tensor(out=st_[:, 0, 0:M], in0=pss[:, 0:M], in1=bt[:, 0, 0:M], op=ALU.add), reads=[('ps', b4)] + job['rdbt'], writes=[('st', b4)])
                        S.op('dve', lambda e: e.tensor_tensor(out=st_[0:nkc, 1, 0:M], in0=pss[0:nkc, 128:128 + M], in1=bt[0:nkc, 1, 0:M], op=ALU.add), reads=[('ps', b4)] + job['rdbt'], writes=[('st2', b4)])
                        S.op('act', lambda e: e.activation(out=pT_[:, 0, 0:M], in_=st_[:, 0, 0:M], func=AF.Exp, bias=job['bprev'], scale=SCALE), reads=[('st', b4), 'mf'], writes=[('pT', b4)])
                        S.op('act', lambda e: e.activation(out=pT_[0:nkc, 1, 0:M], in_=st_[0:nkc, 1, 0:M], func=AF.Exp, scale=SCALE), reads=[('st2', b4)], writes=[('pT2', b4)])

                def stageB(job):
                    i = job['i']
                    b4 = i % NB
                    pso = ps[4 + b4]
                    pT_ = pT[b4]
                    M, nkc = job['M'], job['nkc']

                    def mm2(e):
                        e.matmul(pso[:, 0:M], lhsT=job['vprev'], rhs=pT_[:, 0, 0:M], start=True, stop=False)
                        e.matmul(pso[:, 0:M], lhsT=job['vcur'], rhs=pT_[0:nkc, 1, 0:M], start=False, stop=True)
                        e.matmul(pso[:, 128:128 + M], lhsT=onesq[:, :], rhs=pT_[:, 0, 0:M], start=True, stop=False)
                        return e.matmul(pso[:, 128:128 + M], lhsT=onesq[0:nkc, :], rhs=pT_[0:nkc, 1, 0:M], start=False, stop=True)
                    S.op('pe', mm2, reads=[('pT', b4), ('pT2', b4), 'onesq'] + job['rd2'], writes=[('ps', 4 + b4)])
                    src = pso[:, 0:256].rearrange("p (w a) -> p w a", w=2)[:, :, 0:M]
                    if job['first']:
                        S.op('act', lambda e: e.activation(out=job['dst'], in_=src, func=AF.Copy), reads=[('ps', 4 + b4)], writes=[job['dkey']])
                    else:
                        S.op('dve', lambda e: e.tensor_tensor(out=job['dst'], in0=job['dst'], in1=src, op=ALU.add), reads=[('ps', 4 + b4), job['dkey']], writes=[job['dkey']])

                def submit(job):
                    job['i'] = cnt2[0]
                    cnt2[0] += 1
                    stageA(job)
                    pipe.append(job)
                    if len(pipe) > LAG:
                        stageB(pipe.pop(0))

                def flush():
                    while pipe:
                        stageB(pipe.pop(0))

                ui = 0
                for g in range(3):
                    d = DIL[g]
                    nres, nq = NRES[g], NQ[g]
                    for s_ in range(NSQ):
                        for r in range(nres):
                            u2 = ui % 2
                            ui += 1
                            cb, cbb, ct, vc = ck[u2], ckb[u2], ckt[u2], vcs[u2]
                            S.dma('sp', lambda e, g=g, s_=s_, r=r, d=d, cb=cb: e.dma_start(out=cb, in_=cache[g][li, s_, r:r + 127 * d + 1:d].rearrange("r a h d -> r (a h d)")), writes=[('ck', u2)])
                            S.dma('sp', lambda e, g=g, s_=s_, r=r, d=d, vc=vc, nq=nq: e.dma_start(out=vc[0:nq, :], in_=Vs[g, NP + s_ * 8 + r:NP + s_ * 8 + r + (nq - 1) * d + 1:d, :]), writes=[('vc', u2)])
                            S.op('act', lambda e, cb=cb, cbb=cbb: e.activation(out=cbb[:, 0:1024], in_=cb[:, 0:1024], func=AF.Copy), reads=[('ck', u2)], writes=[('ckbk', u2)])
                            S.op('dve', lambda e, cb=cb, cbb=cbb: e.tensor_copy(out=cbb[:, 1024:2048], in_=cb[:, 1024:2048]), reads=[('ck', u2)], writes=[('ckbv', u2)])
                            for hf in range(2):
                                ptb = ps[hf][:, :].bitcast(BF16)

                                def tr(e, hf=hf, ptb=ptb, cbb=cbb):
                                    last = None
                                    for hh in range(4):
                                        last = e.transpose(ptb[:, hh * 128:hh * 128 + 128], cbb[:, (hf * 4 + hh) * 128:(hf * 4 + hh) * 128 + 128], ident[:, :])
                                    return last
                                S.op('pe', tr, reads=[('ckbk', u2), 'ident'], writes=[('ps', hf)])
                                S.op('act', lambda e, hf=hf, ptb=ptb, ct=ct: e.activation(out=ct[:, hf * 4:hf * 4 + 4, :], in_=ptb[:, 0:512].rearrange("p (h k) -> p h k", h=4), func=AF.Copy), reads=[('ps', hf)], writes=[('ckt', u2, hf)])
                            qc = slice(s_ * 8 + r, s_ * 8 + r + (nq - 1) * d + 1, d)
                            for h in range(8):
                                submit(dict(M=nq, nkc=nq, kprev=ct[:, h, :], kcur=KS[:, g, h, qc], q=QS[:, g, h, qc], bt=BTa[:, g, h], merge=False, bprev=zero1,
                                            vprev=cbb[:, 1024 + h * 128:1024 + h * 128 + 128], vcur=vc[0:nq, h * 128:h * 128 + 128],
                                            rd1=[('ckt', u2, 0), ('ckt', u2, 1), 'QS', 'KS'], rdbt=['BTa'], rd2=[('ckbv', u2), ('vc', u2)],
                                            first=False, dst=accS[:, h, :, qc], dkey='accS'))
                flush()
                S.barrier()
                ar.top = P2
                LD = []
                for _ in range(2):
                    LD.append(dict(Q=ar.bf16(NP), K=ar.bf16(NP), V=ar.bf16(16 * 128).rearrange("p (b d) -> p b d", b=16),
                                   KH=ar.bf16(16 * 128).rearrange("p (b k) -> p b k", b=16), VH=ar.bf16(16 * 128).rearrange("p (b d) -> p b d", b=16),
                                   BT=ar.f32(256).rearrange("p (w a) -> p w a", w=2)))
                hg = [(h, g) for h in range(8) for g in range(3)]

                def loads(k):
                    h, g = hg[k]
                    L = LD[k % 2]
                    lk = k % 2
                    S.dma('sp', lambda e: e.dma_start(out=L['Q'], in_=QTs[g, h, :, 0:NP]), writes=[('LQ', lk)])
                    S.dma('sp', lambda e: e.dma_start(out=L['K'], in_=KTs[g, h, :, 0:NP]), writes=[('LK', lk)])
                    S.dma('sp', lambda e: e.dma_start(out=L['V'], in_=Vs[g, 0:NP, h * 128:h * 128 + 128].rearrange("(b k) d -> k b d", k=128)), writes=[('LV', lk)])
                    S.dma('sp', lambda e: e.dma_start(out=L['BT'], in_=BTs[g, h].rearrange("k (w a) -> k w a", w=2)), writes=[('LBT', lk)])
                    for hb in range(HPB[g]):
                        for kv in range(2):
                            eo = (((HB0[g] + hb) * 8 + h) * 2 + kv) * 128 * 128
                            dst = (L['KH'] if kv == 0 else L['VH'])[:, hb, :]
                            S.dma('pool', lambda e, dst=dst, eo=eo: e.indirect_dma_start(out=dst, out_offset=None, in_=hall[li], in_offset=bass.IndirectOffsetOnAxis(ap=mi[:, 0:1], axis=0), element_offset=eo),
                                  reads=['hall', 'mi'], writes=[('LKH', lk) if kv == 0 else ('LVH', lk)])
                loads(0)
                for k, (h, g) in enumerate(hg):
                    L = LD[k % 2]
                    lk = k % 2
                    d = DIL[g]
                    nb = 16 // d
                    for bi in range(16):
                        if bi == LAG + 1 and k + 1 < len(hg):
                            loads(k + 1)
                        r, n_ = divmod(bi, nb)
                        qc = slice(bi * 128, bi * 128 + 128)
                        c0 = r + n_ * 128 * d
                        dest = slice(c0, c0 + 127 * d + 1, d)
                        if n_ == 0:
                            kprev, vprev, bprev, merge = L['KH'][:, r, :], L['VH'][:, r, :], negf, False
                        else:
                            kprev, vprev, bprev, merge = L['K'][:, (bi - 1) * 128:bi * 128], L['V'][:, bi - 1, :], zero1, True
                        submit(dict(M=128, nkc=128, kprev=kprev, kcur=L['K'][:, qc], q=L['Q'][:, qc], bt=L['BT'], merge=merge, bprev=bprev,
                                    vprev=vprev, vcur=L['V'][:, bi, :], rd1=[('LQ', lk), ('LK', lk), ('LKH', lk)], rdbt=[('LBT', lk)], rd2=[('LV', lk), ('LVH', lk)],
                                    first=(g == 0), dst=acc[:, :, dest], dkey='acc'))
                    if g == 2:
                        flush()
                        S.op('dve', lambda e, h=h: e.tensor_copy(out=acc[:, :, NP:NT], in_=accS[:, h, :, :]), reads=['accS', 'acc'], writes=['acc'])
                        S.op('dve', lambda e: e.reciprocal(out=acc[:, 1, :], in_=acc[:, 1, :]), reads=['acc'], writes=['acc'])
                        S.op('dve', lambda e: e.tensor_tensor(out=acc[:, 0, :], in0=acc[:, 0, :], in1=acc[:, 1, :], op=ALU.mult), reads=['acc'], writes=['acc'])
                        S.op('dve', lambda e, h=h: e.tensor_tensor(out=G[:, h, :], in0=G[:, h, :], in1=acc[:, 0, :], op=ALU.mult), reads=['acc', 'G'], writes=['G'])
                S.barrier()
                Wo = LD[0]['Q'][:, 0:2048].rearrange("p (c n) -> p c n", c=8)
                Wo2 = LD[1]['Q'][:, 0:2048].rearrange("p (c n) -> p c n", c=8)
                Wos = [Wo, Wo2]
                load_w(Wos[0], a_wout[li][:, 0:256], 8, 256, ('Wo', 0))
                for q4 in range(4):
                    if q4 + 1 < 4:
                        load_w(Wos[(q4 + 1) % 2], a_wout[li][:, (q4 + 1) * 256:(q4 + 2) * 256], 8, 256, ('Wo', (q4 + 1) % 2))
                    Wq = Wos[q4 % 2]
                    for cc in range(2):
                        dmc = q4 * 2 + cc
                        for ti, (t0, n) in enumerate(TILES):
                            n = min(n, NT - t0)
                            pi = (dmc * 5 + ti) % 4

                            def mm(e, cc=cc, t0=t0, n=n, pi=pi, Wq=Wq):
                                last = None
                                for hh in range(8):
                                    last = e.matmul(ps[pi][:, 0:n], lhsT=Wq[:, hh, cc * 128:cc * 128 + 128], rhs=G[:, hh, t0:t0 + n], start=(hh == 0), stop=(hh == 7))
                                return last
                            S.op('pe', mm, reads=[('Wo', q4 % 2), 'G'], writes=[('ps', pi)])
                            S.op('dve', lambda e, dmc=dmc, t0=t0, n=n, pi=pi: e.tensor_tensor(out=xT[:, dmc, t0:t0 + n], in0=xT[:, dmc, t0:t0 + n], in1=ps[pi][:, 0:n], op=ALU.add), reads=[('ps', pi), 'xT'], writes=['xT'])
                S.barrier()
            else:
                ar.top = persist_top
                xn = ar.bf16(8 * NX).rearrange("p (c t) -> p c t", c=8)
                gam = ar.f32(8)
                cw = ar.f32(40).rearrange("p (j k) -> p j k", j=10)
                cb_ = ar.f32(10)
                gab = ar.f32(10)
                gxb = ar.f32(10)
                lamc = ar.f32(10)
                h0s = ar.f32(40).rearrange("p (j s) -> p j s", j=10)
                GA = ar.bf16(1280).rearrange("p (j o) -> p j o", j=10)
                GX = ar.bf16(1280).rearrange("p (j o) -> p j o", j=10)
                Wj = ar.bf16(8 * 256).rearrange("p (c n) -> p c n", c=8)
                AGG = ar.f32(100).rearrange("p (j k) -> p j k", j=10)
                XB = ar.f32(LX)
                XC = ar.f32(LX)
                XCb = ar.bf16(LX)
                RA = ar.f32(LX)
                IB = ar.f32(LX)
                TT = XB
                mk_ = ar.top
                sqb = ar.bf16(8 * 512).rearrange("p (c t) -> p c t", c=8)
                rstd = ar.f32(512)
                ar.top = mk_
                HL = ar.f32(LX)
                AC = ar.f32(LX)
                SG = ar.bf16(LX)
                GP = XCb
                CP = ar.bf16(LX)
                xh = ar.f32(24)
                S.dma('sp', lambda e: e.dma_start(out=xsend[li].rearrange("p (c t) -> p c t", c=8), in_=xT[:, :, NP - 3:NP]), reads=['xT'], writes=['xsend'])
                S.barrier()
                S.op('pool', lambda e: e.collective_compute("AllGather", ALU.bypass, replica_groups=ALLG, ins=[xsend[li]], outs=[xall[li]]), reads=['xsend'], writes=['xall'])
                S.dma('pool', lambda e: e.indirect_dma_start(out=xh, out_offset=None, in_=xall[li], in_offset=bass.IndirectOffsetOnAxis(ap=mi[:, 1:2], axis=0)), reads=['xall', 'mi'], writes=['xh'])
                S.op('dve', lambda e: e.tensor_scalar(out=xT[:, :, NT:NX], in0=xh[:, :].rearrange("p (c t) -> p c t", c=8), scalar1=notfirst, scalar2=0.0, op0=ALU.mult, op1=ALU.add), reads=['xh', 'mf'], writes=['xTh'])
                for (dst, src, key) in ((gam, r_norm[li], 'gam'), (cw, r_cw[li], 'cw'), (cb_, r_cb[li], 'cb'), (gab, r_gab[li], 'gab'), (gxb, r_gxb[li], 'gxb'), (lamc, r_lam[li], 'lamc'), (h0s, h0s_in[li], 'h0s')):
                    S.dma('sp', lambda e, dst=dst, src=src: e.dma_start(out=dst, in_=src), writes=[key])
                S.op('act', lambda e: e.activation(out=lamc, in_=lamc, func=AF.Exp, scale=-1.0), reads=['lamc'], writes=['lamc'])
                S.op('act', lambda e: e.activation(out=lamc, in_=lamc, func=AF.Ln, bias=1.0, scale=1.0), reads=['lamc'], writes=['lamc'])
                S.op('dve', lambda e: e.tensor_scalar(out=lamc, in0=lamc, scalar1=-8.0, scalar2=0.0, op0=ALU.mult, op1=ALU.add), reads=['lamc'], writes=['lamc'])
                for (dstw, srcw, key) in ((GA, r_gaw[li], 'GA'), (GX, r_gxw[li], 'GX')):
                    for j0 in range(0, 10, 5):
                        st = XB[:, 0:640].rearrange("p (j o) -> p j o", j=5)
                        S.dma('sp', lambda e, st=st, srcw=srcw, j0=j0: e.dma_start(out=st, in_=srcw[j0:j0 + 5].rearrange("j i o -> i j o")), writes=['XB'])
                        S.op('pool', lambda e, st=st, dstw=dstw, j0=j0: e.tensor_copy(out=dstw[:, j0:j0 + 5, :], in_=st), reads=['XB'], writes=[key])
                rmsnorm(gam)
                S.barrier()
                S.op('pool', lambda e: e.memset(XC[:, 0:3], 0.0), writes=['XC'])
                SEGS = [slice(3, 3 + NP)] + [slice(NP + 3 + 11 * s + 3, NP + 3 + 11 * s + 11) for s in range(NSQ)]
                sview = lambda t: t[:, NP + 3:LX].rearrange("p (s e) -> p s e", e=11)
                for j in range(10):
                    load_w(Wj[:, :, 0:128], r_win[li][:, j * 128:j * 128 + 128], 8, 128, 'Wj0')
                    load_w(Wj[:, :, 128:256], r_win[li][:, 1280 + j * 128:1280 + j * 128 + 128], 8, 128, 'Wj1')
                    S.dma('sp', lambda e, j=j: e.dma_start(out=sview(XB)[:, :, 0:3], in_=cst_in[li][:, j]), writes=['XB'])
                    for which in range(2):
                        for ti, (t0, n) in enumerate(TILES):
                            pi = ti % 2 + 2 * which

                            def mm(e, which=which, t0=t0, n=n, pi=pi):
                                last = None
                                for c in range(8):
                                    last = e.matmul(ps[pi][:, 0:n], lhsT=Wj[:, c, which * 128:which * 128 + 128], rhs=xn[:, c, t0:t0 + n], start=(c == 0), stop=(c == 7))
                                return last
                            S.op('pe', mm, reads=['Wj0', 'Wj1', 'xn'], writes=[('ps', pi)])
                            dstt = XB if which == 0 else SG
                            func = AF.Copy if which == 0 else AF.Silu
                            dkey = 'XB' if which == 0 else 'SG'
                            if ti < 4:
                                S.op('act', lambda e, dstt=dstt, func=func, t0=t0, n=n, pi=pi: e.activation(out=dstt[:, 3 + t0:3 + t0 + n], in_=ps[pi][:, 0:n], func=func), reads=[('ps', pi)], writes=[dkey])
                            else:
                                S.op('act', lambda e, dstt=dstt, func=func, pi=pi: e.activation(out=sview(dstt)[:, :, 3:11], in_=ps[pi][:, 0:32].rearrange("p (s t) -> p s t", t=8), func=func), reads=[('ps', pi)], writes=[dkey])
                                S.op('act', lambda e, dstt=dstt, func=func, pi=pi: e.activation(out=dstt[:, 0:3], in_=ps[pi][:, 32:35], func=func), reads=[('ps', pi)], writes=[dkey])
                    S.dma('sp', lambda e, j=j: e.dma_start(out=cp_out[li, j], in_=XB[:, NP:NP + 3]), reads=['XB'], writes=[('cp', j)])
                    S.dma('sp', lambda e, j=j: e.dma_start(out=cs_out[li, j], in_=sview(XB)[:, :, 8:11]), reads=['XB'], writes=[('cs', j)])
                    n_ = LX - 3
                    S.op('dve', lambda e, j=j: e.tensor_scalar(out=XC[:, 3:LX], in0=XB[:, 0:n_], scalar1=cw[:, j, 0:1], scalar2=cb_[:, j:j + 1], op0=ALU.mult, op1=ALU.add), reads=['XB', 'cw', 'cb'], writes=['XC'])
                    for k in range(1, 4):
                        S.op('dve', lambda e, j=j, k=k: e.scalar_tensor_tensor(out=XC[:, 3:LX], in0=XB[:, k:k + n_], scalar=cw[:, j, k:k + 1], in1=XC[:, 3:LX], op0=ALU.mult, op1=ALU.add), reads=['XB', 'XC', 'cw'], writes=['XC'])
                    S.op('pool', lambda e: e.tensor_copy(out=XCb, in_=XC), reads=['XC'], writes=['XCb'])
                    for which in range(2):
                        Wg = GA if which == 0 else GX
                        bb = gab if which == 0 else gxb
                        dstt = RA if which == 0 else IB
                        dkey = 'RA' if which == 0 else 'IB'
                        for ti, (t0, n) in enumerate(XTILES):
                            pi = 4 + ti % 2 + 2 * which
                            S.op('pe', lambda e, Wg=Wg, j=j, t0=t0, n=n, pi=pi: e.matmul(ps[pi][:, 0:n], lhsT=Wg[:, j, :], rhs=XCb[:, t0:t0 + n], start=True, stop=True), reads=['XCb', 'GA', 'GX'], writes=[('ps', pi)])
                            S.op('act', lambda e, bb=bb, dstt=dstt, j=j, t0=t0, n=n, pi=pi: e.activation(out=dstt[:, t0:t0 + n], in_=ps[pi][:, 0:n], func=AF.Sigmoid, bias=bb[:, j:j + 1], scale=1.0), reads=[('ps', pi), 'gab', 'gxb'], writes=[dkey])
                    S.op('act', lambda e, j=j: e.activation(out=RA, in_=RA, func=AF.Exp, scale=lamc[:, j:j + 1]), reads=['RA', 'lamc'], writes=['RA'])
                    S.op('dve', lambda e: e.tensor_tensor(out=TT, in0=RA, in1=RA, op=ALU.mult), reads=['RA'], writes=['XB'])
                    S.op('act', lambda e: e.activation(out=TT, in_=TT, func=AF.Sqrt, bias=1.0, scale=-1.0), reads=['XB'], writes=['XB'])
                    S.op('dve', lambda e: e.tensor_tensor(out=IB, in0=IB, in1=TT, op=ALU.mult), reads=['IB', 'XB'], writes=['IB'])
                    S.op('dve', lambda e: e.tensor_tensor(out=IB, in0=IB, in1=XC, op=ALU.mult), reads=['IB', 'XC'], writes=['IB'])
                    S.op('pool', lambda e: e.memset(HL, 0.0), writes=['HL'])
                    S.op('pool', lambda e: e.memset(AC, 0.0), writes=['AC'])
                    for sg in SEGS:
                        S.op('dve', lambda e, sg=sg: e.tensor_tensor_scan(out=HL[:, sg], data0=RA[:, sg], data1=IB[:, sg], initial=0.0, op0=ALU.mult, op1=ALU.add), reads=['RA', 'IB'], writes=['HL'])
                        S.op('dve', lambda e, sg=sg: e.tensor_tensor_scan(out=AC[:, sg], data0=RA[:, sg], data1=RA[:, sg], initial=1.0, op0=ALU.mult, op1=ALU.min), reads=['RA'], writes=['AC'])
                    S.op('act', lambda e, j=j: e.activation(out=AGG[:, j, 0:1], in_=AC[:, NP + 2:NP + 3], func=AF.Copy), reads=['AC'], writes=['AGG'])
                    S.op('act', lambda e, j=j: e.activation(out=AGG[:, j, 1:5], in_=sview(AC)[:, :, 10], func=AF.Copy), reads=['AC'], writes=['AGG'])
                    S.op('act', lambda e, j=j: e.activation(out=AGG[:, j, 5:6], in_=HL[:, NP + 2:NP + 3], func=AF.Copy), reads=['HL'], writes=['AGG'])
                    S.op('act', lambda e, j=j: e.activation(out=AGG[:, j, 6:10], in_=sview(HL)[:, :, 10], func=AF.Copy), reads=['HL'], writes=['AGG'])
                    S.op('dve', lambda e: e.tensor_tensor(out=GP, in0=SG, in1=HL, op=ALU.mult), reads=['SG', 'HL'], writes=['XCb'])
                    S.op('dve', lambda e: e.tensor_tensor(out=CP, in0=SG, in1=AC, op=ALU.mult), reads=['SG', 'AC'], writes=['CP'])
                    S.dma('sp', lambda e, j=j: e.dma_start(out=RG[j], in_=GP), reads=['XCb'], writes=[('RG', j)])
                    S.dma('sp', lambda e, j=j: e.dma_start(out=RC[j], in_=CP), reads=['CP'], writes=[('RC', j)])
                AG8 = XB[:, 0:160].rearrange("p (r f) -> p r f", r=8)
                AE = XC[:, 0:80].rearrange("p (r j) -> p r j", r=8)
                HE = XC[:, 80:160].rearrange("p (r j) -> p r j", r=8)
                stc = RA[:, 0:10]
                HIN = IB[:, 0:50].rearrange("p (j s) -> p j s", j=10)
                HF = TT[:, 200:250].rearrange("p (j s) -> p j s", j=10)
                S.barrier()
                S.dma('sp', lambda e: e.dma_start(out=asend[li].rearrange("p (j k) -> p j k", k=2), in_=AGG[:, :, 0:10:5]), reads=['AGG'], writes=['asend'])
                S.barrier()
                S.op('pool', lambda e: e.collective_compute("AllGather", ALU.bypass, replica_groups=ALLG, ins=[asend[li]], outs=[aall[li]]), reads=['asend'], writes=['aall'])
                S.dma('sp', lambda e: e.dma_start(out=AG8, in_=aall[li].rearrange("(r p) f -> p r f", p=128)), reads=['aall'], writes=['AG8'])
                mr = mf[:, 2:10].unsqueeze(2).to_broadcast([128, 8, 10])
                agA = AG8.rearrange("p r (j k) -> p r j k", k=2)[:, :, :, 0]
                agH = AG8.rearrange("p r (j k) -> p r j k", k=2)[:, :, :, 1]
                S.op('dve', lambda e: e.tensor_scalar(out=AE, in0=agA, scalar1=-1.0, scalar2=0.0, op0=ALU.add, op1=ALU.add), reads=['AG8'], writes=['AE'])
                S.op('dve', lambda e: e.tensor_tensor(out=AE, in0=AE, in1=mr, op=ALU.mult), reads=['AE', 'mf'], writes=['AE'])
                S.op('dve', lambda e: e.tensor_scalar(out=AE, in0=AE, scalar1=1.0, scalar2=0.0, op0=ALU.add, op1=ALU.add), reads=['AE'], writes=['AE'])
                S.op('dve', lambda e: e.tensor_tensor(out=HE, in0=agH, in1=mr, op=ALU.mult), reads=['AG8', 'mf'], writes=['HE'])
                S.op('dve', lambda e: e.memset(stc, 0.0), writes=['stc'])
                for r in range(8):
                    S.op('dve', lambda e, r=r: e.tensor_tensor(out=stc, in0=stc, in1=AE[:, r, :], op=ALU.mult), reads=['stc', 'AE'], writes=['stc'])
                    S.op('dve', lambda e, r=r: e.tensor_tensor(out=stc, in0=stc, in1=HE[:, r, :], op=ALU.add), reads=['stc', 'HE'], writes=['stc'])
                S.op('dve', lambda e: e.tensor_copy(out=HIN[:, :, 0], in_=stc), reads=['stc'], writes=['HIN'])
                S.op('dve', lambda e: e.tensor_copy(out=HIN[:, :, 1:5], in_=h0s), reads=['h0s', 'HIN'], writes=['HIN'])
                S.op('dve', lambda e: e.tensor_tensor(out=HF, in0=AGG[:, :, 0:5], in1=HIN, op=ALU.mult), reads=['AGG', 'HIN'], writes=['HF'])
                S.op('dve', lambda e: e.tensor_tensor(out=HF, in0=HF, in1=AGG[:, :, 5:10], op=ALU.add), reads=['AGG', 'HF'], writes=['HF'])
                S.dma('sp', lambda e: e.dma_start(out=hp_out[li], in_=HF[:, :, 0]), reads=['HF'], writes=['hp'])
                S.dma('sp', lambda e: e.dma_start(out=hs_out[li], in_=HF[:, :, 1:5]), reads=['HF'], writes=['hs'])
                GJ = SG
                GSm = XC[:, 256:272].bitcast(BF16)
                Woj = ar.bf16(1024)
                Wov = Woj[:, :].rearrange("p (c n) -> p c n", c=1)
                for j in range(10):
                    S.dma('sp', lambda e, j=j: e.dma_start(out=GP, in_=RG[j]), writes=['XCb'])
                    S.dma('sp', lambda e, j=j: e.dma_start(out=CP, in_=RC[j]), writes=['CP'])
                    load_w(Wov, r_wout[li][j * 128:j * 128 + 128, :], 1, 1024, 'Woj')
                    for si, sg in enumerate(SEGS):
                        S.op('dve', lambda e, j=j, si=si, sg=sg: e.scalar_tensor_tensor(out=GJ[:, sg], in0=CP[:, sg], scalar=HIN[:, j, si:si + 1], in1=GP[:, sg], op0=ALU.mult, op1=ALU.add), reads=['CP', 'XCb', 'HIN'], writes=['GJ'])
                    S.op('pool', lambda e: e.tensor_copy(out=GSm.rearrange("p (s t) -> p s t", t=8), in_=sview(GJ)[:, :, 3:11]), reads=['GJ'], writes=['GSm'])
                    for dmc in range(8):
                        for ti, (t0, n) in enumerate(TILES):
                            pi = (dmc * 5 + ti) % 4
                            if ti < 4:
                                rhs = GJ[:, 3 + t0:3 + t0 + n]
                            else:
                                n = 32
                                rhs = GSm
                            S.op('pe', lambda e, dmc=dmc, rhs=rhs, n=n, pi=pi: e.matmul(ps[pi][:, 0:n], lhsT=Woj[:, dmc * 128:dmc * 128 + 128], rhs=rhs, start=True, stop=True), reads=['Woj', 'GJ', 'GSm'], writes=[('ps', pi)])
                            S.op('dve', lambda e, dmc=dmc, t0=t0, n=n, pi=pi: e.tensor_tensor(out=xT[:, dmc, t0:t0 + n], in0=xT[:, dmc, t0:t0 + n], in1=ps[pi][:, 0:n], op=ALU.add), reads=[('ps', pi), 'xT'], writes=['xT'])
                S.barrier()
        S.dma('sp', lambda e: e.dma_start(out=yT_out, in_=xT[:, :, 0:NT]), reads=['xT'], writes=['yT'])
        S.barrier()
        nc.sync.wait_ge(ccsem, ccn[0])
    return nc


def t5_bucket_np(n):
    n = np.maximum(n, 0)
    max_exact = 16
    nf = np.maximum(n, 1).astype(np.float32)
    large = max_exact + (np.log(nf / np.float32(max_exact)) / np.float32(math.log(2048 / max_exact)) * np.float32(32 - max_exact)).astype(np.int32)
    large = np.minimum(large, 31)
    return np.where(n < max_exact, n, large)


_CACHE = {}


def kernel(x_prompt, x_sample, cache_kv_w128, cache_kv_w512, cache_kv_w2048, state_rglru_h,
           state_rglru_conv, attn_norm, attn_w_in, attn_q_norm, attn_k_norm, attn_w_out, rel_bias,
           rnn_norm, rnn_w_in, rnn_conv_w, rnn_conv_b, rnn_gate_a_w, rnn_gate_a_b, rnn_gate_x_w,
           rnn_gate_x_b, rnn_lambda, rnn_w_out):
    f = lambda a: np.ascontiguousarray(np.asarray(a, dtype=np.float32))
    x_prompt, x_sample = f(x_prompt), f(x_sample)
    caches = [f(cache_kv_w128), f(cache_kv_w512), f(cache_kv_w2048)]
    if 'nc' not in _CACHE:
        _CACHE['nc'] = build_program()
    nc = _CACHE['nc']
    ohr = np.zeros((33, 3, 384), np.float32)
    for g in range(3):
        for e in range(384):
            step = e - 127
            row = int(t5_bucket_np(np.array([step * DIL[g]]))[0]) if 0 <= step <= 128 else 32
            ohr[row, g, 383 - e] = 1.0
    negrow = np.full((1, 24), NEG, np.float32)

    def chan(a):
        a = f(a)
        return np.ascontiguousarray(np.swapaxes(a.reshape(a.shape[:-1] + (10, 128)), -1, -2))

    shared = {
        "a_norm": np.ascontiguousarray(np.swapaxes(f(attn_norm).reshape(2, 8, 128), 1, 2)),
        "a_win": f(attn_w_in),
        "a_qn": np.ascontiguousarray(np.broadcast_to(f(attn_q_norm)[:, None, :], (2, 128, 128))),
        "a_kn": np.ascontiguousarray(np.broadcast_to(f(attn_k_norm)[:, None, :], (2, 128, 128))),
        "a_wout": f(attn_w_out),
        "relb": f(rel_bias),
        "r_norm": np.ascontiguousarray(np.swapaxes(f(rnn_norm).reshape(2, 8, 128), 1, 2)),
        "r_win": f(rnn_w_in),
        "r_cw": np.ascontiguousarray(np.transpose(f(rnn_conv_w).reshape(2, 4, 10, 128), (0, 3, 2, 1))),
        "r_cb": chan(rnn_conv_b),
        "r_gaw": f(rnn_gate_a_w),
        "r_gab": np.ascontiguousarray(np.swapaxes(f(rnn_gate_a_b), 1, 2)),
        "r_gxw": f(rnn_gate_x_w),
        "r_gxb": np.ascontiguousarray(np.swapaxes(f(rnn_gate_x_b), 1, 2)),
        "r_lam": chan(rnn_lambda),
        "r_wout": f(rnn_w_out),
        "ohr": ohr,
        "negrow": negrow,
    }
    sh = f(state_rglru_h)
    sc = f(state_rglru_conv)
    in_maps = []
    for c in range(NCORE):
        b, q = divmod(c, 4)
        xs = np.concatenate([x_prompt[b, q * NP:(q + 1) * NP], x_sample[c * 4:(c + 1) * 4].reshape(NS, 1024)], axis=0)
        xT = np.ascontiguousarray(xs.T.reshape(8, 128, NT).transpose(1, 0, 2))
        m = dict(shared)
        m["xT_in"] = xT
        for g in range(3):
            m["cache%d" % g] = np.ascontiguousarray(caches[g][:, c * 4:(c + 1) * 4])
        hs_ = sh[:, c * 4:(c + 1) * 4].reshape(2, 4, 10, 128)
        m["h0s"] = np.ascontiguousarray(hs_.transpose(0, 3, 2, 1))
        cs_ = sc[:, c * 4:(c + 1) * 4].reshape(2, 4, 3, 10, 128)
        m["cst"] = np.ascontiguousarray(cs_.transpose(0, 4, 3, 1, 2))
        mfv = np.zeros((128, 12), np.float32)
        first = (q == 0)
        mfv[:, 0] = NEG if first else 0.0
        mfv[:, 1] = 0.0 if first else 1.0
        mfv[:, 11] = EPS
        for r in range(NCORE):
            if r // 4 == b and r < c:
                mfv[:, 2 + r] = 1.0
        m["meta_f"] = mfv
        prev = c - 1 if not first else c
        miv = np.zeros((128, 2), np.int32)
        miv[:, 0] = prev * (NHB * 16 * 128) + np.arange(128)
        miv[:, 1] = prev * 128 + np.arange(128)
        m["meta_i"] = miv
        in_maps.append(m)
    res = run_bass_kernel_spmd(nc, in_maps, core_ids=list(range(NCORE))).results
    y_prompt = np.zeros((2, 8192, 1024), np.float32)
    y_sample = np.zeros((32, 8, 1024), np.float32)
    for c in range(NCORE):
        b, q = divmod(c, 4)
        yT = res[c]["yT_out"]
        y = yT.transpose(2, 1, 0).reshape(NT, 1024)
        y_prompt[b, q * NP:(q + 1) * NP] = y[:NP]
        y_sample[c * 4:(c + 1) * 4] = y[NP:].reshape(4, 8, 1024)
    outs = [y_prompt, y_sample]
    for g in range(3):
        outs.append(np.ascontiguousarray(np.stack([res[3]["kvp%d" % g], res[7]["kvp%d" % g]], axis=1)))
        outs.append(np.ascontiguousarray(np.concatenate([res[c]["kvs%d" % g] for c in range(NCORE)], axis=1)))

    def unchan(a):
        return np.ascontiguousarray(np.swapaxes(a, -1, -2).reshape(a.shape[:-2] + (1280,)))
    h_p = np.stack([unchan(res[3]["hp_out"]), unchan(res[7]["hp_out"])], axis=1)
    h_s = np.concatenate([unchan(np.transpose(res[c]["hs_out"], (0, 3, 1, 2))) for c in range(NCORE)], axis=1)
    cpf = lambda a: np.ascontiguousarray(np.transpose(a, (0, 3, 1, 2)).reshape(2, 3, 1280))
    c_p = np.stack([cpf(res[3]["cp_out"]), cpf(res[7]["cp_out"])], axis=1)
    csf = lambda a: np.ascontiguousarray(np.transpose(a, (0, 3, 4, 1, 2)).reshape(2, 4, 3, 1280))
    c_s = np.concatenate([csf(res[c]["cs_out"]) for c in range(NCORE)], axis=1)
    outs += [h_p.astype(np.float32), h_s.astype(np.float32), c_p.astype(np.float32), c_s.astype(np.float32)]
    return tuple(outs)
```
